# Optimizing a Trainium2 kernel written in Bass

```python
import math
import jax
import jax.numpy as jnp
from jax import lax
import numpy as np

D_MODEL = 1024
BATCH = 4
SEQ = 4096
DEPTH = 2
DEC_BATCH = 8
DEC_SEQ = 2048
PAST_LEN = 128

GRID_W = 64
CHUNK = 128
A_WIDTH = D_MODEL // 2
A_GROUPS = 4
A_GROUP_DIM = A_WIDTH // A_GROUPS
NA_HEADS = 8
NA_HEAD_DIM = (D_MODEL - A_WIDTH) // NA_HEADS
NA_WIDTH = NA_HEADS * NA_HEAD_DIM
NA_KH_MAX = 8
NA_KW = 16
DIFF_HEADS = 8
DIFF_HEAD_DIM = D_MODEL // DIFF_HEADS // 2
DIFF_WIDTH = DIFF_HEADS * 2 * DIFF_HEAD_DIM
D_FF = 4 * D_MODEL
Q_BLOCK = 128
LN_EPS = 1e-5
DEEPNORM_ALPHA = (2 * DEPTH) ** 0.25
DEEPNORM_BETA = (8 * DEPTH) ** -0.25

kernel_name = "hybrid_gmlp_natten_diffattn_encoder"


def layer_norm(x, g, b):
    xf = x.astype(jnp.float32)
    mu = jnp.mean(xf, axis=-1, keepdims=True)
    var = jnp.mean(jnp.square(xf - mu), axis=-1, keepdims=True)
    return ((xf - mu) * lax.rsqrt(var + LN_EPS) * g + b).astype(x.dtype)


def rms_norm(x, g):
    xf = x.astype(jnp.float32)
    ms = jnp.mean(jnp.square(xf), axis=-1, keepdims=True)
    return (xf * lax.rsqrt(ms + LN_EPS) * g).astype(x.dtype)


def lambda_init(layer):
    return 0.8 - 0.6 * math.exp(-0.3 * layer)


def chunked_spatial_gating(z, ln_g, ln_b, w_s, b_s):
    B, T, _ = z.shape
    u, v = jnp.split(z, 2, axis=-1)
    v = layer_norm(v, ln_g, ln_b)
    nc = T // CHUNK
    v = v.reshape(B, nc, CHUNK, A_GROUPS, A_GROUP_DIM)
    mixed = jnp.einsum('gts,bnsgc->bntgc', w_s, v) + b_s.T[None, None, :, :, None]
    return u * mixed.reshape(B, T, A_WIDTH)


def neighbourhood_attention(q, k, v, rpb):
    B, T, H, dh = q.shape
    rows = T // GRID_W
    kh = min(NA_KH_MAX, rows)
    kw = NA_KW
    qg = q.reshape(B, rows, GRID_W, H, dh) * (dh ** -0.5)
    kg = k.reshape(B, rows, GRID_W, H, dh)
    vg = v.reshape(B, rows, GRID_W, H, dh)
    cols = jnp.arange(GRID_W)
    c_start = jnp.clip(cols - kw // 2, 0, GRID_W - kw)
    col_idx = c_start[:, None] + jnp.arange(kw)[None, :]
    col_off = col_idx - cols[:, None] + (NA_KW - 1)

    def one_row(r):
        r_start = jnp.clip(r - kh // 2, 0, rows - kh)
        k_rows = lax.dynamic_slice_in_dim(kg, r_start, kh, axis=1)
        v_rows = lax.dynamic_slice_in_dim(vg, r_start, kh, axis=1)
        k_win = k_rows[:, :, col_idx]
        v_win = v_rows[:, :, col_idx]
        q_row = lax.dynamic_index_in_dim(qg, r, axis=1, keepdims=False)
        s = jnp.einsum('bchd,bicjhd->bhcij', q_row, k_win).astype(jnp.float32)
        row_off = r_start + jnp.arange(kh) - r + (NA_KH_MAX - 1)
        bias = rpb[:, row_off][:, :, col_off]
        s = s + jnp.transpose(bias, (0, 2, 1, 3))[None]
        p = jax.nn.softmax(s.reshape(B, H, GRID_W, kh * kw), axis=-1)
        p = p.reshape(B, H, GRID_W, kh, kw).astype(v.dtype)
        return jnp.einsum('bhcij,bicjhd->bchd', p, v_win)

    out = lax.map(one_row, jnp.arange(rows))
    return jnp.transpose(out, (1, 0, 2, 3, 4)).reshape(B, T, H * dh)


def mixer_gmlp_natten(x, w_in, w_out, gate_ln_g, gate_ln_b, w_spatial, b_spatial, na_rpb):
    B, T, _ = x.shape
    h = x @ w_in
    z_a, q, k, v = jnp.split(h, [2 * A_WIDTH, 2 * A_WIDTH + NA_WIDTH, 2 * A_WIDTH + 2 * NA_WIDTH], axis=-1)
    out_a = chunked_spatial_gating(jax.nn.gelu(z_a), gate_ln_g, gate_ln_b, w_spatial, b_spatial)
    shp = (B, T, NA_HEADS, NA_HEAD_DIM)
    out_b = neighbourhood_attention(q.reshape(shp), k.reshape(shp), v.reshape(shp), na_rpb)
    return jnp.concatenate([out_a, out_b], axis=-1) @ w_out


def differential_attention(q, k, v, lam, lam_init):
    B, T, H, _, dh = q.shape
    nb = T // Q_BLOCK
    slopes = jnp.exp2(-8.0 * jnp.arange(1, H + 1, dtype=jnp.float32) / H)
    qb = jnp.transpose((q * (dh ** -0.5)).reshape(B, nb, Q_BLOCK, H, 2, dh), (1, 0, 2, 3, 4, 5))
    k_pos = jnp.arange(T)

    def one_block(args):
        q_blk, i = args
        q_pos = i * Q_BLOCK + jnp.arange(Q_BLOCK)
        dist = jnp.abs(q_pos[:, None] - k_pos[None, :]).astype(jnp.float32)
        bias = -slopes[:, None, None] * dist
        s = jnp.einsum('bqhnd,bkhnd->bhnqk', q_blk, k).astype(jnp.float32) + bias[None, :, None]
        p = jax.nn.softmax(s, axis=-1)
        p = (p[:, :, 0] - lam * p[:, :, 1]).astype(v.dtype)
        return jnp.einsum('bhqk,bkhe->bqhe', p, v)

    o = lax.map(one_block, (qb, jnp.arange(nb)))
    return jnp.transpose(o, (1, 0, 2, 3, 4)).reshape(B, T, H, 2 * dh)


def mixer_diff(x, w_in, w_out, lambda_q1, lambda_k1, lambda_q2, lambda_k2, subln_g, lam_init):
    B, T, _ = x.shape
    h = x @ w_in
    q, k, v = jnp.split(h, 3, axis=-1)
    q = q.reshape(B, T, DIFF_HEADS, 2, DIFF_HEAD_DIM)
    k = k.reshape(B, T, DIFF_HEADS, 2, DIFF_HEAD_DIM)
    v = v.reshape(B, T, DIFF_HEADS, 2 * DIFF_HEAD_DIM)
    lam1 = jnp.exp(jnp.sum((lambda_q1 * lambda_k1).astype(jnp.float32)))
    lam2 = jnp.exp(jnp.sum((lambda_q2 * lambda_k2).astype(jnp.float32)))
    lam = lam1 - lam2 + lam_init
    o = differential_attention(q, k, v, lam, lam_init)
    o = rms_norm(o, subln_g) * (1.0 - lam_init)
    return o.reshape(B, T, DIFF_WIDTH) @ w_out


def squared_relu_mlp(x, w1, w2):
    return jnp.square(jax.nn.relu(x @ w1)) @ w2


def setup_inputs(seed: int = 0) -> dict:
    key = jax.random.key(seed)
    ks = jax.random.split(key, 32)
    f32 = jnp.float32
    d = D_MODEL

    def nrm(k, shape, scale):
        return jax.random.normal(k, shape, f32) * scale

    def gain(k, n):
        return 1.0 + 0.02 * jax.random.normal(k, (n,), f32)

    w_in0 = nrm(ks[2], (d, 2 * A_WIDTH + 3 * NA_WIDTH), d ** -0.5)
    w_in0 = w_in0.at[:, 2 * A_WIDTH + 2 * NA_WIDTH:].multiply(DEEPNORM_BETA)
    w_in1 = nrm(ks[15], (d, 3 * DIFF_WIDTH), d ** -0.5)
    w_in1 = w_in1.at[:, 2 * DIFF_WIDTH:].multiply(DEEPNORM_BETA)
    return {
        "x_prompt": jax.random.normal(ks[0], (BATCH, SEQ, d), f32),
        "x_sample": jax.random.normal(ks[1], (DEC_BATCH, DEC_SEQ, d), f32),
        "l0_w_in": w_in0,
        "l0_w_out": nrm(ks[3], (A_WIDTH + NA_WIDTH, d), (A_WIDTH + NA_WIDTH) ** -0.5 * DEEPNORM_BETA),
        "l0_gate_ln_g": gain(ks[4], A_WIDTH),
        "l0_gate_ln_b": nrm(ks[5], (A_WIDTH,), 0.02),
        "l0_w_spatial": nrm(ks[6], (A_GROUPS, CHUNK, CHUNK), CHUNK ** -0.5),
        "l0_b_spatial": 1.0 + nrm(ks[7], (A_GROUPS, CHUNK), 0.02),
        "l0_na_rpb": nrm(ks[8], (NA_HEADS, 2 * NA_KH_MAX - 1, 2 * NA_KW - 1), 0.02),
        "l0_ln1_g": gain(ks[9], d),
        "l0_ln1_b": nrm(ks[10], (d,), 0.02),
        "l0_w_ff1": nrm(ks[11], (d, D_FF), d ** -0.5 * DEEPNORM_BETA),
        "l0_w_ff2": nrm(ks[12], (D_FF, d), D_FF ** -0.5 * DEEPNORM_BETA),
        "l0_ln2_g": gain(ks[13], d),
        "l0_ln2_b": nrm(ks[14], (d,), 0.02),
        "l1_w_in": w_in1,
        "l1_w_out": nrm(ks[16], (DIFF_WIDTH, d), DIFF_WIDTH ** -0.5 * DEEPNORM_BETA),
        "l1_lambda_q1": nrm(ks[17], (DIFF_HEAD_DIM,), 0.1),
        "l1_lambda_k1": nrm(ks[18], (DIFF_HEAD_DIM,), 0.1),
        "l1_lambda_q2": nrm(ks[19], (DIFF_HEAD_DIM,), 0.1),
        "l1_lambda_k2": nrm(ks[20], (DIFF_HEAD_DIM,), 0.1),
        "l1_subln_g": gain(ks[21], 2 * DIFF_HEAD_DIM),
        "l1_ln1_g": gain(ks[22], d),
        "l1_ln1_b": nrm(ks[23], (d,), 0.02),
        "l1_w_ff1": nrm(ks[24], (d, D_FF), d ** -0.5 * DEEPNORM_BETA),
        "l1_w_ff2": nrm(ks[25], (D_FF, d), D_FF ** -0.5 * DEEPNORM_BETA),
        "l1_ln2_g": gain(ks[26], d),
        "l1_ln2_b": nrm(ks[27], (d,), 0.02),
    }


def reference(x_prompt, x_sample, l0_w_in, l0_w_out, l0_gate_ln_g, l0_gate_ln_b, l0_w_spatial,
              l0_b_spatial, l0_na_rpb, l0_ln1_g, l0_ln1_b, l0_w_ff1, l0_w_ff2, l0_ln2_g, l0_ln2_b,
              l1_w_in, l1_w_out, l1_lambda_q1, l1_lambda_k1, l1_lambda_q2, l1_lambda_k2, l1_subln_g,
              l1_ln1_g, l1_ln1_b, l1_w_ff1, l1_w_ff2, l1_ln2_g, l1_ln2_b):
    ffn = [(l0_w_ff1, l0_w_ff2), (l1_w_ff1, l1_w_ff2)]
    ln1 = [(l0_ln1_g, l0_ln1_b), (l1_ln1_g, l1_ln1_b)]
    ln2 = [(l0_ln2_g, l0_ln2_b), (l1_ln2_g, l1_ln2_b)]

    def trunk(x):
        for layer in range(DEPTH):
            if layer % 2 == 0:
                mix = mixer_gmlp_natten(x, l0_w_in, l0_w_out, l0_gate_ln_g, l0_gate_ln_b,
                                        l0_w_spatial, l0_b_spatial, l0_na_rpb)
            else:
                mix = mixer_diff(x, l1_w_in, l1_w_out, l1_lambda_q1, l1_lambda_k1, l1_lambda_q2,
                                 l1_lambda_k2, l1_subln_g, lambda_init(layer))
            x = layer_norm(DEEPNORM_ALPHA * x + mix, *ln1[layer])
            x = layer_norm(DEEPNORM_ALPHA * x + squared_relu_mlp(x, *ffn[layer]), *ln2[layer])
        return x

    y_prompt = trunk(x_prompt)
    y_sample = trunk(x_sample)
    return (y_prompt, y_sample)
```

```python
import contextlib
import math
import numpy as np
import concourse.bass as bass
import concourse.mybir as mybir
from concourse.bass_utils import run_bass_kernel_spmd

F32 = mybir.dt.float32
BF16 = mybir.dt.bfloat16
AF = mybir.ActivationFunctionType
ALU = mybir.AluOpType

D = 1024
T = 4096
NT = 32
DFF = 4096
ALPHA = 4.0 ** 0.25
LAM_INIT = 0.8 - 0.6 * math.exp(-0.3)
LN_EPS = 1e-5
NEG = -30000.0
ENGS = ("pe", "act", "dve", "pool", "sp")
EPOCH = 30000
NSEM = 80


class Sched:
    def __init__(self, nc, sems):
        self.nc = nc
        self.sempool = sems
        self.semmap = {}
        self.ops = {e: [] for e in ENGS}
        self.cnt = {e: 0 for e in ENGS}
        self.waited = {e: {} for e in ENGS}
        self.lastw = {}
        self.readers = {}
        self.dmacnt = {}
        self.ninst = 0

    def _sem(self, key):
        if key not in self.semmap:
            self.semmap[key] = self.sempool[len(self.semmap)]
        return key

    def op(self, eng, fn, reads=(), writes=(), sig=True, dma=None):
        deps = []
        for r in reads:
            t = self.lastw.get(r)
            if t is not None:
                deps.append(t)
        for w in writes:
            t = self.lastw.get(w)
            if t is not None:
                deps.append(t)
            rd = self.readers.get(w)
            if rd:
                deps.extend(rd.values())
        wl = self.waited[eng]
        need = {}
        for (sk, val, teng, isdma) in deps:
            if teng == eng and eng == "pe" and not isdma:
                continue
            if isdma:
                val = self.dmacnt[sk[1]]
            if need.get(sk, 0) < val:
                need[sk] = val
        for sk, val in need.items():
            if wl.get(sk, 0) < val:
                wl[sk] = val
                self.ops[eng].append(("w", sk, val))
        if dma is not None:
            c = self.dmacnt.get(dma, 0) + 16
            self.dmacnt[dma] = c
            tok = (self._sem(("d", dma)), c, eng, True)
            inc = (tok[0], 16)
        else:
            n = self.cnt[eng] + 1
            if sig:
                self.cnt[eng] = n
            idx = n - 1
            tok = (self._sem(("e", eng, idx // EPOCH)), idx % EPOCH + 1, eng, False)
            inc = (tok[0], 1) if sig else None
        self.ops[eng].append(("o", fn, inc))
        self.ninst += 1
        for w in writes:
            self.lastw[w] = tok
            self.readers[w] = {}
        for r in reads:
            d = self.readers.setdefault(r, {})
            k = tok[0]
            if k not in d or d[k][1] < tok[1]:
                d[k] = tok

    def fence(self, eng, keys):
        wl = self.waited[eng]
        for k in keys:
            t = self.lastw.get(k)
            if t is None:
                continue
            if wl.get(t[0], 0) < t[1]:
                wl[t[0]] = t[1]
                self.ops[eng].append(("w", t[0], t[1]))

    def check(self):
        sem = getattr(self, "_simsem", {})
        pos = {e: 0 for e in ENGS}
        progress = True
        while progress:
            progress = False
            for e in ENGS:
                lst = self.ops[e]
                while pos[e] < len(lst):
                    it = lst[pos[e]]
                    if it[0] == "w":
                        if sem.get(it[1], 0) < it[2]:
                            break
                    elif it[2] is not None:
                        sem[it[2][0]] = sem.get(it[2][0], 0) + it[2][1]
                    pos[e] += 1
                    progress = True
        for e in ENGS:
            if pos[e] < len(self.ops[e]):
                it = self.ops[e][pos[e]]
                raise RuntimeError("schedule deadlock: engine %s stuck at %d/%d waiting %s >= %s (have %s)" % (
                    e, pos[e], len(self.ops[e]), it[1], it[2], sem.get(it[1], 0)))
        self._simsem = sem

    def emit(self):
        wl = self.waited["sp"]
        for key, cnt in self.dmacnt.items():
            sk = ("d", key)
            if wl.get(sk, 0) < cnt:
                wl[sk] = cnt
                self.ops["sp"].append(("w", sk, cnt))
        self.check()
        nc = self.nc
        ops = self.ops
        self.ops = {e: [] for e in ENGS}
        semmap = self.semmap

        def run(engobj, lst):
            for it in lst:
                if it[0] == "w":
                    engobj.wait_ge(semmap[it[1]], it[2])
                else:
                    ins = it[1](engobj)
                    if it[2] is not None:
                        ins.then_inc(semmap[it[2][0]], it[2][1])

        with nc.Block() as block:
            @block.tensor
            def _(e):
                run(e, ops["pe"])

            @block.scalar
            def _(e):
                run(e, ops["act"])

            @block.vector
            def _(e):
                run(e, ops["dve"])

            @block.gpsimd
            def _(e):
                run(e, ops["pool"])

            @block.sync
            def _(e):
                run(e, ops["sp"])


def MM(out, lhsT, rhs, start=True, stop=True, skip=False):
    if skip:
        return lambda e: e.matmul(out, lhsT, rhs, start=start, stop=stop, skip_group_check=True)
    return lambda e: e.matmul(out, lhsT, rhs, start=start, stop=stop)


def TR(out, in_, ident):
    return lambda e: e.transpose(out, in_, ident)


def ACT(out, in_, func, bias=None, scale=1.0, accum=None):
    kw = {}
    if bias is not None:
        kw["bias"] = bias
    if accum is not None:
        kw["accum_out"] = accum
    return lambda e: e.activation(out=out, in_=in_, func=func, scale=scale, **kw)


def TT(out, a, b, op):
    return lambda e: e.tensor_tensor(out, a, b, op)


def TS(out, a, s1, s2, op0, op1=None):
    if op1 is None:
        return lambda e: e.tensor_scalar(out, a, s1, None, op0)
    return lambda e: e.tensor_scalar(out, a, s1, s2, op0, op1)


def STT(out, in0, scalar, in1, op0, op1):
    return lambda e: e.scalar_tensor_tensor(out, in0, scalar, in1, op0, op1)


def CP(out, in_):
    return lambda e: e.tensor_copy(out, in_)


def DMA(out, in_):
    return lambda e: e.dma_start(out=out, in_=in_)


def MEMSET(ap, v):
    return lambda e: e.memset(ap, v)


def _natten_struct():
    def valid(typ, r, kr):
        segrows = 64 if typ == "P" else 32
        if r // segrows != kr // segrows:
            return False
        base = (r // segrows) * segrows
        rs = min(max((r - base) - 4, 0), segrows - 8) + base
        return rs <= kr < rs + 8

    JL = []
    for m in range(NT):
        js = set()
        for typ in ("P", "S"):
            for j in range(NT):
                if any(valid(typ, 2 * m + f, 2 * j + e) for e in (0, 1) for f in (0, 1)):
                    js.add(j)
        js = sorted(js)
        assert all(abs(j - m) <= 3 for j in js) and len(js) <= 6
        JL.append(js)
    vb = {}
    for typ in ("P", "S"):
        tab = np.zeros((128, NT * 6 * 2), np.float32)
        for m in range(NT):
            for jj, j in enumerate(JL[m]):
                for f in (0, 1):
                    for e in (0, 1):
                        if not valid(typ, 2 * m + f, 2 * j + e):
                            tab[e * 64:(e + 1) * 64, (m * 6 + jj) * 2 + f] = NEG
        vb[typ] = tab
    return JL, vb


def _natten_tt(rpb):
    H = rpb.shape[0]
    c = np.arange(64)
    cs = np.clip(c - 8, 0, 48)
    cp = np.arange(64)
    inwin = (cp[:, None] >= cs[None, :]) & (cp[:, None] < cs[None, :] + 16)
    coff = np.clip(cp[:, None] - c[None, :] + 15, 0, 30)
    G = np.full((H, 17, 64, 64), NEG, np.float32)
    for rho in range(15):
        g = rpb[:, rho][:, coff]
        G[:, rho + 1] = np.where(inwin[None], g, np.float32(NEG))
    tt = np.empty((128, H, 7, 128), np.float32)
    for i0 in range(7):
        rho0 = 2 * i0 + 1
        for e in (0, 1):
            for f in (0, 1):
                rho = rho0 + e - f
                tt[e * 64:(e + 1) * 64, :, i0, f * 64:(f + 1) * 64] = np.transpose(G[:, rho + 1], (1, 0, 2))
    return np.ascontiguousarray(tt.reshape(128, H * 7 * 128))


def _attn_tables(typ):
    a = np.arange(512, dtype=np.float32)[None, :]
    b = np.arange(128, dtype=np.float32)[:, None]
    ab = np.ascontiguousarray(np.broadcast_to(a - b, (128, 512))).astype(np.float32)
    dg = np.stack([np.abs(a - b - 128.0 * j) for j in range(4)], axis=1).astype(np.float32)
    vb2 = np.zeros((128, 8 * 8 * 32), np.float32)
    for h in range(8):
        slope = 2.0 ** (-(h + 1))
        for qb in range(8):
            for kt in range(32):
                delta = 128.0 * (4 * qb - kt)
                if kt < 4 * qb:
                    v = -slope * delta
                elif kt >= 4 * qb + 4:
                    v = slope * delta
                else:
                    v = 0.0
                if typ == "S" and (qb // 4) != (kt // 16):
                    v = NEG
                vb2[:, (h * 8 + qb) * 32 + kt] = v
    return ab, np.ascontiguousarray(dg.reshape(128, 2048)), vb2


import os
DEBUG = bool(int(os.environ.get("KDEBUG", "0")))


def build_program():
    JL, VBT = _natten_struct()
    nc = bass.Bass("TRN2", target_bir_lowering=False)

    def din(name, shape, dt=F32):
        return nc.dram_tensor(name, list(shape), dt, kind="ExternalInput").ap()

    def dscr(name, shape, dt):
        if DEBUG and name in ("xmid", "x1s", "attno", "xmid1"):
            return nc.dram_tensor(name, list(shape), dt, kind="ExternalOutput").ap()
        return nc.dram_tensor(name, list(shape), dt).ap()

    x_in = din("x", [T, D])
    w_in0 = din("l0_w_in", [D, 2560])
    w_out0 = din("l0_w_out", [D, D])
    w_ff1 = [din("l0_w_ff1", [D, DFF]), din("l1_w_ff1", [D, DFF])]
    w_ff2 = [din("l0_w_ff2", [DFF, D]), din("l1_w_ff2", [DFF, D])]
    w_in1 = din("l1_w_in", [D, 3072])
    w_out1 = din("l1_w_out", [D, D])
    gate_gb = din("gate_gb", [128, 1024])
    wsT_in = din("wsT", [128, 512])
    bsT_in = din("bsT", [128, 4])
    tt_in = din("tt", [128, 8 * 7 * 128])
    vb_in = din("vb", [128, NT * 12])
    lnp = [[din("l%d_ln%d" % (l, i), [128, 2048]) for i in (1, 2)] for l in (0, 1)]
    lamv = din("lamv", [128, 256])
    subg_in = din("subg", [128, 128])
    ab_in = din("ab", [128, 512])
    dg_in = din("dg", [128, 2048])
    vb2_in = din("vb2", [128, 2048])
    y_out = nc.dram_tensor("y", [T, D], F32, kind="ExternalOutput").ap()

    w1b = [dscr("w1b%d" % l, [D, DFF], BF16) for l in (0, 1)]
    w2b = [dscr("w2b%d" % l, [DFF, D], BF16) for l in (0, 1)]
    win1b = dscr("win1b", [D, 3072], BF16)
    xmid = dscr("xmid", [T, D], F32)
    xmid1 = dscr("xmid1", [T, D], F32) if DEBUG else xmid
    x1s = dscr("x1s", [T, D], F32)
    qts = dscr("qts", [8, 128, T], BF16)
    attno = dscr("attno", [T, D], BF16)

    with contextlib.ExitStack() as top:
        sems = [top.enter_context(nc.semaphore("s%d" % i)) for i in range(NSEM)]
        S = Sched(nc, sems)
        PA = nc.alloc_psum_tensor("PA", [128, 1024], F32)
        PB = nc.alloc_psum_tensor("PB", [128, 1024], F32)
        PC = nc.alloc_psum_tensor("PC", [128, 512], F32)
        PD = nc.alloc_psum_tensor("PD", [128, 512], F32)
        PE_ = nc.alloc_psum_tensor("PE", [128, 512], F32)
        PT = nc.alloc_psum_tensor("PT", [128, 1024], BF16)
        PAB = [(PA, "PA"), (PB, "PB")]
        P3 = [(PC, "PC"), (PD, "PD"), (PE_, "PE")]
        identf = nc.alloc_sbuf_tensor("identf", [128, 128], F32)
        ident = nc.alloc_sbuf_tensor("ident", [128, 128], BF16)
        i8 = nc.alloc_sbuf_tensor("i8", [128, 128], BF16)
        mhalf = nc.alloc_sbuf_tensor("mhalf", [128, 8], F32)

        S.op("pool", lambda e: e.iota(identf[:], [[1, 128]], base=0, channel_multiplier=-1,
                                      allow_small_or_imprecise_dtypes=True), writes=["identf"])
        S.op("dve", lambda e: e.tensor_single_scalar(ident[:], identf[:], 0.0, ALU.is_equal),
             reads=["identf"], writes=["ident"])
        S.op("dve", TS(i8[:], ident[:], 8.0, None, ALU.mult), reads=["ident"], writes=["i8"])
        S.op("dve", MEMSET(mhalf[:], -0.5), writes=["mhalf"])

        def conv_weight(dst, src, rows, key):
            for r in range(0, rows, 128):
                S.op("pool", DMA(dst[r:r + 128, :], src[r:r + 128, :]), writes=[(key, r // 128)],
                     dma="cv_" + key)

        def layer_norm_tile(yb, dst, gb, kq, ybk, dstk, st):
            bst, mv, ve, rs = st
            for hh in (0, 1):
                S.op("dve", (lambda o, i: (lambda e: e.bn_stats(o, i)))(bst[:, hh * 6:(hh + 1) * 6], yb[:, hh * 512:(hh + 1) * 512]),
                     reads=[ybk], writes=[kq + "bst%d" % hh])
            S.op("dve", (lambda o, i: (lambda e: e.bn_aggr(o, i)))(mv[:, 0:2], bst[:, 0:12]),
                 reads=[kq + "bst0", kq + "bst1"], writes=[kq + "mv"])
            S.op("dve", TS(ve[:, 0:1], mv[:, 1:2], LN_EPS, None, ALU.add), reads=[kq + "mv"], writes=[kq + "ve"])
            S.op("pool", TT(rs[:, 0:1], ve[:, 0:1], mhalf[:, 0:1], ALU.pow), reads=[kq + "ve", "mhalf"], writes=[kq + "rs"])
            S.op("dve", STT(yb[:, :], yb[:, :], mv[:, 0:1], gb[:, 0:1024], ALU.subtract, ALU.mult),
                 reads=[ybk, kq + "mv", "lngb"], writes=[ybk])
            S.op("dve", STT(dst, yb[:, :], rs[:, 0:1], gb[:, 1024:2048], ALU.mult, ALU.add),
                 reads=[ybk, kq + "rs", "lngb"], writes=[dstk])

        def transpose8(src_bf, srck, dst_ap, dstk, n=8, evac="dve", scale=None):
            for c in range(n):
                S.op("pe", TR(PT[:, c * 128:(c + 1) * 128], src_bf[:, c * 128:(c + 1) * 128], ident[:]),
                     reads=[srck, "ident"], writes=["PT"], sig=(c == n - 1))
            src = PT[:, 0:n * 128].rearrange("p (c t) -> p c t", c=n)
            if evac == "act":
                S.op("act", ACT(dst_ap, src, AF.Copy, scale=(1.0 if scale is None else scale)), reads=["PT"], writes=[dstk])
            else:
                S.op("dve", CP(dst_ap, src), reads=["PT"], writes=[dstk])

        def mixer_out(t, catT, catk, wout, woutk, xres, xresk, gb, dst_rows, bufs, pidx):
            yb, xo, st = bufs
            PW, pk = PAB[pidx % 2]
            for half in (0, 1):
                for k in range(8):
                    S.op("pe", MM(PW[:, half * 512:(half + 1) * 512], catT[:, k, :], wout[:, k, half * 512:(half + 1) * 512],
                                  start=(k == 0), stop=(k == 7)),
                         reads=[catk, woutk], writes=[pk], sig=(half == 1 and k == 7))
            ybt, ybk = yb[t % 2], "yb%d" % (t % 2)
            S.op("dve", STT(ybt[:, :], xres, ALPHA, PW[:, :], ALU.mult, ALU.add), reads=[xresk, pk], writes=[ybk])
            xot, xok = xo[t % 2], "xo%d" % (t % 2)
            layer_norm_tile(ybt, xot[:, :], gb, "lnA", ybk, xok, st)
            S.op("sp", DMA(dst_rows, xot[:, :]), reads=[xok], writes=[("xmid", t)], dma="st_xo%d" % (t % 2))

        with contextlib.ExitStack() as ph:
            def sb(name, shape, dt):
                return ph.enter_context(nc.sbuf_tensor(name, list(shape), dt))
            win0 = sb("win0", [128, 8, 2560], BF16)
            wout0 = sb("wout0", [128, 8, 1024], BF16)
            tts = sb("tts", [128, 8, 7, 128], BF16)
            wsT = sb("wsTs", [128, 4, 128], BF16)
            bsT = sb("bsTs", [128, 4], F32)
            ggb = sb("ggb", [128, 1024], F32)
            vb = sb("vbs", [128, NT * 12], F32)
            gb1 = sb("gb1", [128, 2048], F32)
            xr = [sb("xr%d" % i, [128, 1024], F32) for i in range(5)]
            xb = [sb("xb%d" % i, [128, 1024], BF16) for i in range(2)]
            xT = [sb("xT%d" % i, [128, 8, 128], BF16) for i in range(2)]
            qT = [sb("qT%d" % i, [128, 4, 256], BF16) for i in range(8)]
            ntmp = [sb("ntmp%d" % i, [128, 8, 128], F32) for i in range(2)]
            kT = [sb("kT%d" % i, [128, 4, 128], BF16) for i in range(8)]
            va = [sb("va%d" % i, [128, 8, 65], BF16) for i in range(8)]
            g1 = sb("g1", [128, 1024], F32)
            g2 = sb("g2", [128, 1024], F32)
            gz = [sb("gz%d" % i, [128, 1024], F32) for i in range(2)]
            vn = sb("vn", [128, 512], F32)
            vln = [sb("vln%d" % i, [128, 512], BF16) for i in range(2)]
            oa = sb("oa", [128, 512], BF16)
            catT = [sb("catT%d" % i, [128, 8, 128], BF16) for i in range(4)]
            ptb = [sb("ptb%d" % i, [128, 8, 128], BF16) for i in range(2)]
            ob = sb("ob", [128, 512], BF16)
            rec = sb("rec", [128, 8], F32)
            yb = [sb("yb%d" % i, [128, 1024], F32) for i in range(2)]
            xo = [sb("xo%d" % i, [128, 1024], F32) for i in range(2)]
            st1 = (sb("bst1", [128, 12], F32), sb("mv1", [128, 2], F32), sb("ve1", [128, 1], F32), sb("rs1", [128, 1], F32))
            st0 = (sb("bst0", [128, 12], F32), sb("mv0", [128, 2], F32), sb("ve0", [128, 1], F32), sb("rs0", [128, 1], F32))

            w3 = w_in0.rearrange("(k p) c -> p k c", p=128)
            for k in range(8):
                S.op("pool", DMA(win0[:, k, :], w3[:, k, :]), writes=["win0"], dma="ld_win0")
            S.op("pool", DMA(tts[:].rearrange("p h i q -> p (h i q)"), tt_in[:, :]), writes=["tts"], dma="ld_tts")
            S.op("pool", DMA(wsT[:].rearrange("p g t -> p (g t)"), wsT_in[:, :]), writes=["wsT"], dma="ld_wsT")
            w3o = w_out0.rearrange("(k p) c -> p k c", p=128)
            for k in range(8):
                S.op("pool", DMA(wout0[:, k, :], w3o[:, k, :]), writes=["wout0"], dma="ld_wout0")
            S.op("sp", DMA(bsT[:], bsT_in[:, :]), writes=["bsT"], dma="ld_c0")
            S.op("sp", DMA(ggb[:], gate_gb[:, :]), writes=["ggb"], dma="ld_c1")
            S.op("sp", DMA(vb[:], vb_in[:, :]), writes=["vb"], dma="ld_c2")
            S.op("sp", DMA(gb1[:], lnp[0][0][:, :]), writes=["lngb"], dma="ld_c3")
            for i in range(8):
                S.op("dve", MEMSET(va[i][:, :, 64:65], 1.0), writes=["va1_%d" % i])
                S.op("dve", MEMSET(qT[i][:, :, :], 0.0), writes=["qT%d" % i])
            conv_weight(w1b[0], w_ff1[0], D, "w1b0")
            conv_weight(w2b[0], w_ff2[0], DFF, "w2b0")
            conv_weight(win1b, w_in1, D, "win1b")
            conv_weight(w1b[1], w_ff1[1], D, "w1b1")
            conv_weight(w2b[1], w_ff2[1], DFF, "w2b1")

            def x_load(t):
                S.op("sp", DMA(xr[t % 5][:], x_in[t * 128:(t + 1) * 128, :]), writes=["xr%d" % (t % 5)], dma="ld_xr%d" % (t % 5))

            def stage_A1(t):
                xrt, xrk = xr[t % 5], "xr%d" % (t % 5)
                if t == 0:
                    x_load(0)
                if t + 1 < NT:
                    x_load(t + 1)
                xbt, xbk = xb[t % 2], "xb%d" % (t % 2)
                S.op("act", ACT(xbt[:], xrt[:], AF.Copy), reads=[xrk], writes=[xbk])
                xTt, xTk = xT[t % 2], "xT%d" % (t % 2)
                transpose8(xbt, xbk, xTt[:], xTk, evac="act")
                PZ, pzk = PAB[t % 2]
                for half in (0, 1):
                    for k in range(8):
                        S.op("pe", MM(PZ[:, half * 512:(half + 1) * 512], xTt[:, k, :], win0[:, k, half * 512:(half + 1) * 512],
                                      start=(k == 0), stop=(k == 7)),
                             reads=[xTk, "win0"], writes=[pzk], sig=(half == 1 and k == 7))
                gzt, gzk = gz[t % 2], "gz%d" % (t % 2)
                S.op("act", ACT(g1[:], PZ[:, :], AF.Square, scale=math.sqrt(0.044715)), reads=[pzk], writes=["g1"])
                S.op("dve", STT(g2[:], g1[:], 1.0, PZ[:, :], ALU.add, ALU.mult), reads=["g1", pzk], writes=["g2"])
                S.op("act", ACT(g1[:], g2[:], AF.Tanh, scale=math.sqrt(2.0 / math.pi)), reads=["g2"], writes=["g1"])
                S.op("dve", STT(gzt[:], g1[:], 1.0, PZ[:, :], ALU.add, ALU.mult), reads=["g1", pzk], writes=[gzk])
                for which, base, dst, dk, (PQ, pqk) in (("q", 1024, qT[t % 8], "qT%d" % (t % 8), P3[0]), ("k", 1536, kT[t % 8], "kT%d" % (t % 8), P3[1])):
                    for cc in range(4):
                        for k in range(8):
                            S.op("pe", MM(PQ[:, cc * 128:(cc + 1) * 128], win0[:, k, base + cc * 128: base + (cc + 1) * 128], xTt[:, k, :],
                                          start=(k == 0), stop=(k == 7)),
                                 reads=["win0", xTk], writes=[pqk], sig=(cc == 3 and k == 7))
                    if which == "q":
                        S.op("dve", CP(dst[0:64, :, 0:128], PQ[0:64, :].rearrange("p (c t) -> p c t", c=4)), reads=[pqk, dk], writes=[dk + "a"])
                        S.op("dve", CP(dst[64:128, :, 128:256], PQ[64:128, :].rearrange("p (c t) -> p c t", c=4)), reads=[pqk, dk], writes=[dk + "b"])
                    else:
                        S.op("act", ACT(dst[:], PQ[:, :].rearrange("p (c t) -> p c t", c=4), AF.Copy), reads=[pqk], writes=[dk])
                PV_, pvk = P3[2]
                for k in range(8):
                    S.op("pe", MM(PV_[:, :], xTt[:, k, :], win0[:, k, 2048:2560], start=(k == 0), stop=(k == 7)),
                         reads=["win0", xTk], writes=[pvk], sig=(k == 7))
                S.op("act", ACT(va[t % 8][:, :, 0:64], PV_[:, :].rearrange("p (h d) -> p h d", h=8), AF.Copy),
                     reads=[pvk], writes=["va%d" % (t % 8)])
                bst, mv, ve, rs = st0
                S.op("dve", (lambda e: e.bn_stats(bst[:, 0:6], gzt[:, 512:1024])), reads=[gzk], writes=["g_bst"])
                S.op("dve", (lambda e: e.bn_aggr(mv[:, 0:2], bst[:, 0:6])), reads=["g_bst"], writes=["g_mv"])
                S.op("dve", TS(ve[:, 0:1], mv[:, 1:2], 4.0 * LN_EPS, None, ALU.add), reads=["g_mv"], writes=["g_ve"])
                S.op("pool", TT(rs[:, 0:1], ve[:, 0:1], mhalf[:, 0:1], ALU.pow), reads=["g_ve", "mhalf"], writes=["g_rs"])
                S.op("dve", STT(vn[:], gzt[:, 512:1024], mv[:, 0:1], ggb[:, 0:512], ALU.subtract, ALU.mult),
                     reads=[gzk, "g_mv", "ggb"], writes=["vn"])
                S.op("dve", STT(vln[t % 2][:], vn[:], rs[:, 0:1], ggb[:, 512:1024], ALU.mult, ALU.add),
                     reads=["vn", "g_rs", "ggb"], writes=["vln%d" % (t % 2)])

            def stage_A2(t):
                gzt, gzk = gz[t % 2], "gz%d" % (t % 2)
                vl, vlk = vln[t % 2], "vln%d" % (t % 2)
                for g in range(4):
                    S.op("pe", MM(PC[:, g * 128:(g + 1) * 128], wsT[:, g, :], vl[:, g * 128:(g + 1) * 128]),
                         reads=["wsT", vlk], writes=["PC"], sig=(g == 3))
                for g in range(4):
                    S.op("dve", STT(oa[:, g * 128:(g + 1) * 128], PC[:, g * 128:(g + 1) * 128], bsT[:, g:g + 1],
                                    gzt[:, g * 128:(g + 1) * 128], ALU.add, ALU.mult),
                         reads=["PC", "bsT", gzk], writes=["oa"])
                transpose8(oa, "oa", catT[t % 4][:, 0:4, :], "catA%d" % (t % 4), n=4, evac="act", scale=0.5)

            def stage_N(m):
                J = JL[m]
                qk = "qT%d" % (m % 8)

                def nat_S(jj):
                    j = J[jj]
                    PS, psk = PAB[jj % 2]
                    for cc in range(4):
                        S.op("pe", MM(PS[:, cc * 256:(cc + 1) * 256], kT[j % 8][:, cc, :], qT[m % 8][:, cc, :]),
                             reads=["kT%d" % (j % 8), qk, qk + "a", qk + "b"], writes=[psk], sig=(cc == 3))

                def nat_exp(jj):
                    j = J[jj]
                    i0 = j - m + 3
                    PS, psk = PAB[jj % 2]
                    pt, ptk = ptb[jj % 2], "ptb%d" % (jj % 2)
                    nt_, ntk = ntmp[jj % 2], "ntmp%d" % (jj % 2)
                    S.op("dve", STT(nt_[:, :, :], tts[:, :, i0, :], 8.0, PS[:, :].rearrange("p (h q) -> p h q", h=8), ALU.mult, ALU.add),
                         reads=["tts", psk], writes=[ntk])
                    c0 = (m * 6 + jj) * 2
                    same = all(np.array_equal(VBT[typ][:, c0], VBT[typ][:, c0 + 1]) for typ in ("P", "S"))
                    if same:
                        S.op("act", ACT(pt[:, :, :], nt_[:, :, :], AF.Exp, bias=vb[:, c0:c0 + 1], scale=0.125),
                             reads=[ntk, "vb"], writes=[ptk + "f0", ptk + "f1"])
                    else:
                        for f in (0, 1):
                            S.op("act", ACT(pt[:, :, f * 64:(f + 1) * 64], nt_[:, :, f * 64:(f + 1) * 64], AF.Exp,
                                            bias=vb[:, c0 + f:c0 + f + 1], scale=0.125),
                                 reads=[ntk, "vb"], writes=[ptk + "f%d" % f])

                def nat_PV(jj):
                    j = J[jj]
                    pt, ptk = ptb[jj % 2], "ptb%d" % (jj % 2)
                    for h in range(8):
                        PO, pok = (PD, "PD") if h < 4 else (PE_, "PE")
                        c0 = (h % 4) * 65
                        S.op("pe", MM(PO[:, c0:c0 + 65], pt[:, h, :], va[j % 8][:, h, 0:65], start=(jj == 0 and h % 4 == 0), stop=(jj == len(J) - 1), skip=True),
                             reads=[ptk + "f0", ptk + "f1", "va%d" % (j % 8), "va1_%d" % (j % 8)], writes=[pok],
                             sig=(h == 3 or h == 7))

                nat_S(0)
                if len(J) > 1:
                    nat_S(1)
                for jj in range(len(J)):
                    nat_exp(jj)
                    if jj + 2 < len(J):
                        nat_S(jj + 2)
                    nat_PV(jj)
                for half, (PO, pok) in enumerate(((PD, "PD"), (PE_, "PE"))):
                    pov = PO[:, 0:260].rearrange("p (h d) -> p h d", h=4)
                    S.op("dve", (lambda o, i: (lambda e: e.reciprocal(o, i)))(rec[:, half * 4:(half + 1) * 4], pov[:, :, 64]),
                         reads=[pok], writes=["rec%d" % half])
                    S.op("dve", TT(ob[:, half * 256:(half + 1) * 256].rearrange("p (h d) -> p h d", h=4), pov[:, :, 0:64],
                                   rec[:, half * 4:(half + 1) * 4].unsqueeze(2).broadcast_to([128, 4, 64]), ALU.mult),
                         reads=[pok, "rec%d" % half], writes=["ob"])
                transpose8(ob, "ob", catT[m % 4][:, 4:8, :], "catB%d" % (m % 4), n=4, evac="act")

            def _mixer(m):
                ybt, ybk = yb[m % 2], "yb%d" % (m % 2)
                PW, pk = PAB[m % 2]
                ca, cbk = "catA%d" % (m % 4), "catB%d" % (m % 4)
                for half in (0, 1):
                    for k in range(8):
                        S.op("pe", MM(PW[:, half * 512:(half + 1) * 512], catT[m % 4][:, k, :], wout0[:, k, half * 512:(half + 1) * 512],
                                      start=(k == 0), stop=(k == 7)),
                             reads=[ca, cbk, "wout0"], writes=[pk], sig=(half == 1 and k == 7))
                S.op("dve", STT(ybt[:, :], xr[m % 5][:, :], ALPHA, PW[:, :], ALU.mult, ALU.add),
                     reads=["xr%d" % (m % 5), pk], writes=[ybk])
                xot, xok = xo[m % 2], "xo%d" % (m % 2)
                layer_norm_tile(ybt, xot[:, :], gb1, "lnA", ybk, xok, st1)
                S.op("sp", DMA(xmid[m * 128:(m + 1) * 128, :], xot[:, :]), reads=[xok], writes=[("xmid", m)], dma="st_xo%d" % (m % 2))

            for t in range(NT + 3):
                if t < NT:
                    stage_A1(t)
                if t - 3 >= 0:
                    stage_N(t - 3)
                if t < NT:
                    stage_A2(t)
                if t - 3 >= 0:
                    _mixer(t - 3)
            S.emit()

        def ffn_phase(l, src, srckey, dst, dstkey):
            with contextlib.ExitStack() as ph:
                def sb(name, shape, dt):
                    return ph.enter_context(nc.sbuf_tensor("f%d_%s" % (l, name), list(shape), dt))
                w2s = sb("w2s", [128, 32, 1024], BF16)
                gb2 = sb("gb2", [128, 2048], F32)
                xm = [sb("xm%d" % i, [128, 4, 1024], F32) for i in range(2)]
                xb2 = [sb("xb%d" % i, [128, 1024], BF16) for i in range(2)]
                xT = [sb("xT%d" % i, [128, 8, 512], BF16) for i in range(2)]
                w1buf = [sb("w1buf%d" % i, [128, 8, 512], BF16) for i in range(3)]
                hT = sb("hT", [128, 32, 512], BF16)
                rl = [sb("rl%d" % i, [128, 512], F32) for i in range(2)]
                yb = [sb("yb%d" % i, [128, 1024], F32) for i in range(2)]
                xo = [sb("xo%d" % i, [128, 1024], F32) for i in range(2)]
                st = (sb("bst", [128, 12], F32), sb("mv", [128, 2], F32), sb("ve", [128, 1], F32), sb("rs", [128, 1], F32))
                S.op("sp", DMA(gb2[:], lnp[l][1][:, :]), writes=["lngb"], dma="ld_c3")
                w2v = w2b[l].rearrange("(c p) n -> p c n", p=128)
                for c in range(0, 32, 4):
                    S.op("sp", DMA(w2s[:, c:c + 4, :], w2v[:, c:c + 4, :]), reads=[("w2b%d" % l, i) for i in range(32)],
                         writes=["w2s"], dma="ld_w2s")
                w1v = w1b[l].rearrange("(k p) f -> p k f", p=128)
                wcount = [0]
                def blk_load(B):
                    for tl in range(4):
                        t = B * 4 + tl
                        S.op("sp", DMA(xm[B % 2][:, tl, :], src[t * 128:(t + 1) * 128, :]), reads=[(srckey, t)], writes=["xm%d_%d" % (B % 2, tl)],
                             dma="ld_xm%d" % (B % 2))

                def blk_prep(B):
                    xmb, xmk = xm[B % 2], "xm%d" % (B % 2)
                    xTb, xTk = xT[B % 2], "fxT%d" % (B % 2)
                    for tl in range(4):
                        xbt, xbk = xb2[tl % 2], "fxb%d" % (tl % 2)
                        S.op("act", ACT(xbt[:], xmb[:, tl, :], AF.Copy), reads=[xmk + "_%d" % i for i in range(4)], writes=[xbk])
                        transpose8(xbt, xbk, xTb[:, :, tl * 128:(tl + 1) * 128], xTk + "_%d" % tl)

                blk_load(0)
                for B in range(8):
                    xmb, xmk = xm[B % 2], "xm%d" % (B % 2)
                    xTb, xTk = xT[B % 2], "fxT%d" % (B % 2)
                    blk_prep(B)
                    xTkeys = [xTk + "_%d" % i for i in range(4)]
                    for fg in range(8):
                        wi = wcount[0] % 3
                        wcount[0] += 1
                        S.op("sp", DMA(w1buf[wi][:], w1v[:, :, fg * 512:(fg + 1) * 512]),
                             reads=[("w1b%d" % l, i) for i in range(8)], writes=["w1buf%d" % wi], dma="ld_w1buf%d" % wi)
                        for fc in range(4):
                            f = fg * 4 + fc
                            ps, psk = P3[f % 3]
                            for k in range(8):
                                S.op("pe", MM(ps[:, :], w1buf[wi][:, k, fc * 128:(fc + 1) * 128], xTb[:, k, :], start=(k == 0), stop=(k == 7)),
                                     reads=["w1buf%d" % wi] + xTkeys, writes=[psk], sig=(k == 7))
                            r, rk = rl[f % 2], "rl%d" % (f % 2)
                            S.op("act", ACT(r[:], ps[:, :], AF.Relu), reads=[psk], writes=[rk])
                            S.op("dve" if f % 2 == 0 else "pool", TT(hT[:, f, :], r[:], r[:], ALU.mult), reads=[rk], writes=[("hT", f)])
                    hkeys = [("hT", f) for f in range(32)]
                    if B + 1 < 8:
                        blk_load(B + 1)
                    for tl in range(4):
                        t = B * 4 + tl
                        PW, pk = PAB[tl % 2]
                        for half in (0, 1):
                            for f in range(32):
                                S.op("pe", MM(PW[:, half * 512:(half + 1) * 512], hT[:, f, tl * 128:(tl + 1) * 128], w2s[:, f, half * 512:(half + 1) * 512],
                                              start=(f == 0), stop=(f == 31)),
                                     reads=hkeys + ["w2s"] if f == 0 else [], writes=[pk], sig=(half == 1 and f == 31))
                        ybt, ybk = yb[tl % 2], "fyb%d" % (tl % 2)
                        S.op("dve", STT(ybt[:, :], xmb[:, tl, :], ALPHA, PW[:, :], ALU.mult, ALU.add), reads=[xmk + "_%d" % i for i in range(4)] + [pk], writes=[ybk])
                        xot, xok = xo[tl % 2], "fxo%d" % (tl % 2)
                        layer_norm_tile(ybt, xot[:, :], gb2, "lnF", ybk, xok, st)
                        S.op("sp", DMA(dst[t * 128:(t + 1) * 128, :], xot[:, :]), reads=[xok], writes=[(dstkey, t)], dma="st_fxo%d" % (tl % 2))
                S.emit()

        ffn_phase(0, xmid, "xmid", x1s, "x1s")

        with contextlib.ExitStack() as ph2:
            KT = ph2.enter_context(nc.sbuf_tensor("KT", [128, 8, T], BF16))
            VA = ph2.enter_context(nc.sbuf_tensor("VA", [128, 32, 8, 129], BF16))
            S.op("pool", MEMSET(VA[:, :, :, 128:129], 1.0), writes=["VA1"])
            with contextlib.ExitStack() as ph:
                def sb(name, shape, dt):
                    return ph.enter_context(nc.sbuf_tensor("a_" + name, list(shape), dt))
                xm2 = [sb("xm%d" % i, [128, 4, 1024], F32) for i in range(2)]
                xb2 = [sb("xb%d" % i, [128, 1024], BF16) for i in range(2)]
                xT = [sb("xT%d" % i, [128, 8, 512], BF16) for i in range(2)]
                wbuf = [sb("wbuf%d" % i, [128, 8, 512], BF16) for i in range(3)]

                def a_load(B):
                    for tl in range(4):
                        t = B * 4 + tl
                        S.op("sp", DMA(xm2[B % 2][:, tl, :], x1s[t * 128:(t + 1) * 128, :]), reads=[("x1s", t)], writes=["axm%d_%d" % (B % 2, tl)],
                             dma="ld_axm%d" % (B % 2))

                def a_prep(B):
                    xm = xm2[B % 2]
                    for tl in range(4):
                        xbt, xbk = xb2[tl % 2], "axb%d" % (tl % 2)
                        S.op("act", ACT(xbt[:], xm[:, tl, :], AF.Copy), reads=["axm%d_%d" % (B % 2, i) for i in range(4)], writes=[xbk])
                        transpose8(xbt, xbk, xT[B % 2][:, :, tl * 128:(tl + 1) * 128], "axT%d_%d" % (B % 2, tl))

                a_load(0)
                a_prep(0)
                qst = [sb("qst%d" % i, [128, 512], BF16) for i in range(2)]
                wv = win1b.rearrange("(k p) c -> p k c", p=128)
                wc = 0
                qc = 0
                for B in range(8):
                    xTb, xTk = xT[B % 2], "axT%d" % (B % 2)
                    if B + 1 < 8:
                        a_load(B + 1)
                    xTkeys = [xTk + "_%d" % i for i in range(4)]
                    for g in range(6):
                        wi = wc % 3
                        wc += 1
                        S.op("sp", DMA(wbuf[wi][:], wv[:, :, g * 512:(g + 1) * 512]), reads=[("win1b", i) for i in range(8)],
                             writes=["wbuf%d" % wi], dma="ld_wbuf%d" % wi)
                        if g == 4 and B + 1 < 8:
                            a_prep(B + 1)
                        if g < 4:
                            for cc in range(4):
                                h = (g % 2) * 4 + cc
                                ps, psk = P3[(g * 4 + cc) % 3]
                                for k in range(8):
                                    S.op("pe", MM(ps[:, :], wbuf[wi][:, k, cc * 128:(cc + 1) * 128], xTb[:, k, :], start=(k == 0), stop=(k == 7)),
                                         reads=["wbuf%d" % wi] + xTkeys, writes=[psk], sig=(k == 7))
                                if g < 2:
                                    qi = qc % 2
                                    qc += 1
                                    S.op("act", ACT(qst[qi][:], ps[:, :], AF.Copy), reads=[psk], writes=["qst%d" % qi])
                                    S.op("sp", DMA(qts[h, :, B * 512:(B + 1) * 512], qst[qi][:]), reads=["qst%d" % qi], writes=[("qts", h, B)],
                                         dma="st_qst%d" % qi)
                                else:
                                    S.op("dve", CP(KT[:, h, B * 512:(B + 1) * 512], ps[:, :]), reads=[psk], writes=[("KT", h, B)])
                        else:
                            for tl in range(4):
                                t = B * 4 + tl
                                PW, pk = PAB[tl % 2]
                                for k in range(8):
                                    S.op("pe", MM(PW[:, 0:512], xTb[:, k, tl * 128:(tl + 1) * 128], wbuf[wi][:, k, :], start=(k == 0), stop=(k == 7)),
                                         reads=["wbuf%d" % wi] + xTkeys, writes=[pk], sig=(k == 7))
                                eng = "act" if tl % 2 == 0 else "dve"
                                h0 = (g - 4) * 4
                                src = PW[:, 0:512].rearrange("p (h d) -> p h d", h=4)
                                if eng == "act":
                                    S.op("act", ACT(VA[:, t, h0:h0 + 4, 0:128], src, AF.Copy), reads=[pk], writes=[("VA", t, g)])
                                else:
                                    S.op("dve", CP(VA[:, t, h0:h0 + 4, 0:128], src), reads=[pk], writes=[("VA", t, g)])
                S.emit()

            with contextlib.ExitStack() as ph:
                def sb(name, shape, dt):
                    return ph.enter_context(nc.sbuf_tensor("b_" + name, list(shape), dt))
                ab = sb("ab", [128, 512], F32)
                dg = sb("dg", [128, 4, 512], F32)
                vb2 = sb("vb2", [128, 2048], F32)
                lamt = sb("lamt", [128, 256], F32)
                lamp = sb("lamp", [128, 128], F32)
                lsc = sb("lsc", [128, 8], F32)
                subg = sb("subg", [128, 128], F32)
                qtb = [sb("qtb%d" % i, [128, 512], BF16) for i in range(3)]
                tmp = [sb("tmp%d" % i, [128, 1024], F32) for i in range(3)]
                pts = [sb("pts%d" % i, [128, 1024], BF16) for i in range(4)]
                r12 = sb("r12", [128, 8], F32)
                od = sb("od", [128, 4, 128], F32)
                junk = sb("junk", [128, 128], F32)
                ssq = sb("ssq", [128, 4], F32)
                rsq = sb("rsq", [128, 4], F32)
                obf = [sb("obf%d" % i, [128, 4, 128], BF16) for i in range(2)]
                S.op("sp", DMA(ab[:], ab_in[:, :]), writes=["ab"], dma="ld_c0")
                S.op("sp", DMA(dg[:].rearrange("p j a -> p (j a)"), dg_in[:, :]), writes=["dg"], dma="ld_c1")
                S.op("sp", DMA(vb2[:], vb2_in[:, :]), writes=["vb2"], dma="ld_c2")
                S.op("sp", DMA(lamt[:], lamv[:, :]), writes=["lamt"], dma="ld_c3")
                S.op("sp", DMA(subg[:], subg_in[:, :]), writes=["subg"], dma="ld_c4")
                S.op("dve", TT(lamp[:, 0:64], lamt[:, 0:64], lamt[:, 64:128], ALU.mult), reads=["lamt"], writes=["lamp0"])
                S.op("dve", TT(lamp[:, 64:128], lamt[:, 128:192], lamt[:, 192:256], ALU.mult), reads=["lamt"], writes=["lamp1"])
                S.op("dve", (lambda e: e.reduce_sum(lsc[:, 0:2], lamp[:, :].rearrange("p (a b) -> p a b", a=2), mybir.AxisListType.X)),
                     reads=["lamp0", "lamp1"], writes=["lsc01"])
                S.op("act", ACT(lsc[:, 2:4], lsc[:, 0:2], AF.Exp), reads=["lsc01"], writes=["lsc23"])
                S.op("dve", TT(lsc[:, 4:5], lsc[:, 3:4], lsc[:, 2:3], ALU.subtract), reads=["lsc23"], writes=["lsc4"])
                S.op("dve", TS(lsc[:, 4:5], lsc[:, 4:5], -LAM_INIT, None, ALU.add), reads=["lsc4"], writes=["lsc4"])
                S.op("dve", TS(subg[:], subg[:], 1.0 - LAM_INIT, None, ALU.mult), reads=["subg"], writes=["subg"])

                accs = []
                for i in range(8):
                    P, pk = P3[i // 3]
                    c0 = (i % 3) * 129
                    accs.append((P[:, c0:c0 + 129], pk))
                units = [(h, qb, kt) for h in range(8) for qb in range(8) for kt in range(32)]
                NU = len(units)
                qslot = {}

                def q_load(h, qb):
                    i = h * 8 + qb
                    q_t, q_k = qtb[i % 3], "qtb%d" % (i % 3)
                    S.op("sp", DMA(q_t[:], qts[h, :, qb * 512:(qb + 1) * 512]), reads=[("qts", h, qb)], writes=[q_k], dma="ld_" + q_k)

                def s_stage(u):
                    h, qb, kt = units[u]
                    slope = 2.0 ** (-(h + 1))
                    i = h * 8 + qb
                    q_t, q_k = qtb[i % 3], "qtb%d" % (i % 3)
                    if kt == 16 and i + 1 < 64:
                        q_load((i + 1) // 8, (i + 1) % 8)
                    PS, psk = PAB[u % 2]
                    S.op("pe", MM(PS[:, 0:512], KT[0:64, h, kt * 128:(kt + 1) * 128], q_t[0:64, :]),
                         reads=[("KT", h, kt // 4), q_k], writes=[psk], sig=False)
                    S.op("pe", MM(PS[:, 512:1024], KT[64:128, h, kt * 128:(kt + 1) * 128], q_t[64:128, :]),
                         reads=[("KT", h, kt // 4), q_k], writes=[psk], sig=True)
                    if kt < 4 * qb:
                        tab, c, tk = ab[:, :], -8.0 * slope, "ab"
                    elif kt >= 4 * qb + 4:
                        tab, c, tk = ab[:, :], 8.0 * slope, "ab"
                    else:
                        tab, c, tk = dg[:, kt - 4 * qb, :], -8.0 * slope, "dg"
                    tm, tmk = tmp[u % 3], "tmp%d" % (u % 3)
                    S.op("dve", STT(tm[:, :].rearrange("p (m a) -> p m a", m=2), tab.unsqueeze(1).broadcast_to([128, 2, 512]), c,
                                    PS[:, :].rearrange("p (m a) -> p m a", m=2), ALU.mult, ALU.add),
                         reads=[tk, psk], writes=[tmk])
                    col = (h * 8 + qb) * 32 + kt
                    p_t, p_k = pts[u % 4], "pts%d" % (u % 4)
                    S.op("act", ACT(p_t[:, :], tm[:, :], AF.Exp, bias=vb2[:, col:col + 1], scale=0.125),
                         reads=[tmk, "vb2"], writes=[p_k])

                def pv_stage(u):
                    h, qb, kt = units[u]
                    p_t, p_k = pts[u % 4], "pts%d" % (u % 4)
                    for mp in (0, 1):
                        for sbk in range(4):
                            a_ap, a_k = accs[mp * 4 + sbk]
                            S.op("pe", MM(a_ap, p_t[:, mp * 512 + sbk * 128: mp * 512 + (sbk + 1) * 128], VA[:, kt, h, 0:129],
                                          start=(kt == 0 and (mp * 4 + sbk) % 3 == 0), stop=(kt == 31), skip=True),
                                 reads=[p_k, ("VA", kt, 4 + h // 4), "VA1"], writes=[a_k], sig=(mp == 1 and sbk == 3))

                q_load(0, 0)
                s_stage(0)
                s_stage(1)
                for u in range(NU):
                    h, qb, kt = units[u]
                    if u + 2 < NU:
                        s_stage(u + 2)
                    pv_stage(u)
                    if kt == 31:
                        for bi, (P, pk) in enumerate(P3):
                            n = 3 if bi < 2 else 2
                            pv = P[:, 0:n * 129].rearrange("p (i d) -> p i d", i=n)
                            S.op("dve", (lambda o, i: (lambda e: e.reciprocal(o, i)))(r12[:, bi * 3: bi * 3 + n], pv[:, :, 128]),
                                 reads=[pk], writes=["r12_%d" % bi])
                        rk = ["r12_0", "r12_1", "r12_2"]
                        S.op("dve", TS(r12[:, 4:8], r12[:, 4:8], lsc[:, 4:5], None, ALU.mult), reads=rk + ["lsc4"], writes=["r12b"])
                        S.op("dve", MEMSET(ssq[:, :], 0.0), writes=["ssq%d" % i for i in range(4)])
                        for sbk in range(4):
                            a1, k1 = accs[sbk]
                            a2, k2 = accs[4 + sbk]
                            S.op("dve", TS(od[:, sbk, :], a1[:, 0:128], r12[:, sbk:sbk + 1], None, ALU.mult), reads=[k1] + rk, writes=["od%d" % sbk])
                            S.op("dve", STT(od[:, sbk, :], a2[:, 0:128], r12[:, 4 + sbk:5 + sbk], od[:, sbk, :], ALU.mult, ALU.add),
                                 reads=[k2, "r12b", "od%d" % sbk], writes=["od%d" % sbk])
                            S.op("act", ACT(junk[:], od[:, sbk, :], AF.Square, accum=ssq[:, sbk:sbk + 1]), reads=["od%d" % sbk],
                                 writes=["junk", "ssq%d" % sbk])
                        sk = ["ssq%d" % i for i in range(4)]
                        S.op("dve", TS(rsq[:, :], ssq[:, :], 1.0 / 128.0, LN_EPS, ALU.mult, ALU.add), reads=sk, writes=["rsq"])
                        S.op("pool", TT(rsq[:, :], rsq[:, :], mhalf[:, 0:4], ALU.pow), reads=["rsq", "mhalf"], writes=["rsq"])
                        o_t, o_k = obf[(h * 8 + qb) % 2], "obf%d" % ((h * 8 + qb) % 2)
                        for sbk in range(4):
                            S.op("dve", STT(o_t[:, sbk, :], od[:, sbk, :], rsq[:, sbk:sbk + 1], subg[:, :], ALU.mult, ALU.mult),
                                 reads=["od%d" % sbk, "rsq", "subg"], writes=[o_k + "_%d" % sbk])
                        dstv = attno[qb * 512:(qb + 1) * 512, h * 128:(h + 1) * 128].rearrange("(s p) c -> p s c", p=128)
                        S.op("sp", DMA(dstv, o_t[:, :, :]), reads=[o_k + "_%d" % i for i in range(4)],
                             writes=[("attno", qb * 4 + i, h) for i in range(4)], dma="st_" + o_k)
                S.emit()

        with contextlib.ExitStack() as ph:
            def sb(name, shape, dt):
                return ph.enter_context(nc.sbuf_tensor("c_" + name, list(shape), dt))
            wout1 = sb("wout1", [128, 8, 1024], BF16)
            gb1 = sb("gb1", [128, 2048], F32)
            ao = [sb("ao%d" % i, [128, 1024], BF16) for i in range(3)]
            oT = [sb("oT%d" % i, [128, 8, 128], BF16) for i in range(2)]
            x1t = [sb("x1t%d" % i, [128, 1024], F32) for i in range(3)]
            yb = [sb("yb%d" % i, [128, 1024], F32) for i in range(2)]
            xo = [sb("xo%d" % i, [128, 1024], F32) for i in range(2)]
            st = (sb("bst", [128, 12], F32), sb("mv", [128, 2], F32), sb("ve", [128, 1], F32), sb("rs", [128, 1], F32))
            w3o = w_out1.rearrange("(k p) c -> p k c", p=128)
            for k in range(8):
                S.op("pool", DMA(wout1[:, k, :], w3o[:, k, :]), writes=["wout1"], dma="ld_wout1")
            S.op("sp", DMA(gb1[:], lnp[1][0][:, :]), writes=["lngb"], dma="ld_c3")
            def c_load(t):
                S.op("sp", DMA(ao[t % 3][:], attno[t * 128:(t + 1) * 128, :]), reads=[("attno", t, hh) for hh in range(8)], writes=["ao%d" % (t % 3)], dma="ld_ao%d" % (t % 3))
                S.op("sp", DMA(x1t[t % 3][:], x1s[t * 128:(t + 1) * 128, :]), reads=[("x1s", t)], writes=["x1t%d" % (t % 3)], dma="ld_x1t%d" % (t % 3))

            c_load(0)
            c_load(1)
            for t in range(NT):
                if t + 2 < NT:
                    c_load(t + 2)
                a_t, a_k = ao[t % 3], "ao%d" % (t % 3)
                x_t, x_k = x1t[t % 3], "x1t%d" % (t % 3)
                transpose8(a_t, a_k, oT[t % 2][:], "oT%d" % (t % 2), evac="act")
                PW, pk = PAB[t % 2]
                for half in (0, 1):
                    for k in range(8):
                        S.op("pe", MM(PW[:, half * 512:(half + 1) * 512], oT[t % 2][:, k, :], wout1[:, k, half * 512:(half + 1) * 512],
                                      start=(k == 0), stop=(k == 7)),
                             reads=["oT%d" % (t % 2), "wout1"], writes=[pk], sig=(half == 1 and k == 7))
                ybt, ybk = yb[t % 2], "cyb%d" % (t % 2)
                S.op("dve", STT(ybt[:, :], x_t[:, :], ALPHA, PW[:, :], ALU.mult, ALU.add), reads=[x_k, pk], writes=[ybk])
                xot, xok = xo[t % 2], "cxo%d" % (t % 2)
                layer_norm_tile(ybt, xot[:, :], gb1, "lnC", ybk, xok, st)
                S.op("sp", DMA(xmid1[t * 128:(t + 1) * 128, :], xot[:, :]), reads=[xok], writes=[("xmid", t)], dma="st_cxo%d" % (t % 2))
            S.emit()

        ffn_phase(1, xmid1, "xmid", y_out, "y")
        S.fence("sp", [("y", t) for t in range(NT)])
        S.emit()
    return nc


_CACHE = {}


def kernel(x_prompt, x_sample, l0_w_in, l0_w_out, l0_gate_ln_g, l0_gate_ln_b, l0_w_spatial, l0_b_spatial, l0_na_rpb,
           l0_ln1_g, l0_ln1_b, l0_w_ff1, l0_w_ff2, l0_ln2_g, l0_ln2_b, l1_w_in, l1_w_out, l1_lambda_q1, l1_lambda_k1,
           l1_lambda_q2, l1_lambda_k2, l1_subln_g, l1_ln1_g, l1_ln1_b, l1_w_ff1, l1_w_ff2, l1_ln2_g, l1_ln2_b):
    f = lambda a: np.ascontiguousarray(np.asarray(a, dtype=np.float32))
    x_prompt, x_sample = f(x_prompt), f(x_sample)
    if "nc" not in _CACHE:
        _CACHE["nc"] = build_program()
    nc = _CACHE["nc"]
    _, vbt = _natten_struct()

    def rep(*vs):
        v = np.concatenate([f(a).reshape(-1) for a in vs])
        return np.ascontiguousarray(np.broadcast_to(v[None, :], (128, v.size)))

    shared = {
        "l0_w_in": f(l0_w_in), "l0_w_out": f(l0_w_out), "l0_w_ff1": f(l0_w_ff1), "l0_w_ff2": f(l0_w_ff2),
        "l1_w_in": f(l1_w_in), "l1_w_out": f(l1_w_out), "l1_w_ff1": f(l1_w_ff1), "l1_w_ff2": f(l1_w_ff2),
        "gate_gb": rep(l0_gate_ln_g, l0_gate_ln_b),
        "wsT": np.ascontiguousarray(np.transpose(f(l0_w_spatial), (2, 0, 1)).reshape(128, 512)),
        "bsT": np.ascontiguousarray(f(l0_b_spatial).T),
        "tt": _natten_tt(f(l0_na_rpb)),
        "l0_ln1": rep(l0_ln1_g, l0_ln1_b), "l0_ln2": rep(l0_ln2_g, l0_ln2_b),
        "l1_ln1": rep(l1_ln1_g, l1_ln1_b), "l1_ln2": rep(l1_ln2_g, l1_ln2_b),
        "lamv": rep(l1_lambda_q1, l1_lambda_k1, l1_lambda_q2, l1_lambda_k2),
        "subg": rep(l1_subln_g),
    }
    tabs = {typ: _attn_tables(typ) for typ in ("P", "S")}
    in_maps = []
    for c in range(8):
        typ = "P" if c < 4 else "S"
        xc = x_prompt[c] if c < 4 else x_sample[2 * (c - 4):2 * (c - 4) + 2].reshape(T, D)
        m = dict(shared)
        m["x"] = np.ascontiguousarray(xc)
        m["vb"] = vbt[typ]
        m["ab"], m["dg"], m["vb2"] = tabs[typ]
        in_maps.append(m)
    res = run_bass_kernel_spmd(nc, in_maps, core_ids=list(range(8)))
    if DEBUG:
        _CACHE["dbg"] = res.results
    ys = [np.asarray(r["y"], dtype=np.float32) for r in res.results]
    y_prompt = np.stack(ys[0:4], axis=0)
    y_sample = np.stack(ys[4:8], axis=0).reshape(8, 2048, D)
    return (y_prompt, y_sample)
```

```python
import contextlib
import math
import numpy as np
import concourse.bass as bass
import concourse.mybir as mybir
from concourse.bass_utils import run_bass_kernel_spmd

F32 = mybir.dt.float32
BF16 = mybir.dt.bfloat16
AF = mybir.ActivationFunctionType
ALU = mybir.AluOpType

D = 1024
T = 4096
NT = 32
DFF = 4096
ALPHA = 4.0 ** 0.25
LAM_INIT = 0.8 - 0.6 * math.exp(-0.3)
LN_EPS = 1e-5
NEG = -30000.0
ENGS = ("pe", "act", "dve", "pool", "sp")
EPOCH = 30000
NSEM = 80
SKIP_NATS = 100.0


class Sched:
    def __init__(self, nc, sems):
        self.nc = nc
        self.sempool = sems
        self.semmap = {}
        self.ops = {e: [] for e in ENGS}
        self.cnt = {e: 0 for e in ENGS}
        self.waited = {e: {} for e in ENGS}
        self.lastw = {}
        self.readers = {}
        self.dmacnt = {}
        self.ninst = 0

    def _sem(self, key):
        if key not in self.semmap:
            self.semmap[key] = self.sempool[len(self.semmap)]
        return key

    def op(self, eng, fn, reads=(), writes=(), sig=True, dma=None):
        deps = []
        for r in reads:
            t = self.lastw.get(r)
            if t is not None:
                deps.append(t)
        for w in writes:
            t = self.lastw.get(w)
            if t is not None:
                deps.append(t)
            rd = self.readers.get(w)
            if rd:
                deps.extend(rd.values())
        wl = self.waited[eng]
        need = {}
        for (sk, val, teng, isdma) in deps:
            if teng == eng and eng == "pe" and not isdma:
                continue
            if isdma:
                val = self.dmacnt[sk[1]]
            if need.get(sk, 0) < val:
                need[sk] = val
        for sk, val in need.items():
            if wl.get(sk, 0) < val:
                wl[sk] = val
                self.ops[eng].append(("w", sk, val))
        if dma is not None:
            c = self.dmacnt.get(dma, 0) + 16
            self.dmacnt[dma] = c
            tok = (self._sem(("d", dma)), c, eng, True)
            inc = (tok[0], 16)
        else:
            n = self.cnt[eng] + 1
            if sig:
                self.cnt[eng] = n
            idx = n - 1
            tok = (self._sem(("e", eng, idx // EPOCH)), idx % EPOCH + 1, eng, False)
            inc = (tok[0], 1) if sig else None
        self.ops[eng].append(("o", fn, inc))
        self.ninst += 1
        for w in writes:
            self.lastw[w] = tok
            self.readers[w] = {}
        for r in reads:
            d = self.readers.setdefault(r, {})
            k = tok[0]
            if k not in d or d[k][1] < tok[1]:
                d[k] = tok

    def fence(self, eng, keys):
        wl = self.waited[eng]
        for k in keys:
            t = self.lastw.get(k)
            if t is None:
                continue
            if wl.get(t[0], 0) < t[1]:
                wl[t[0]] = t[1]
                self.ops[eng].append(("w", t[0], t[1]))

    def check(self):
        sem = getattr(self, "_simsem", {})
        pos = {e: 0 for e in ENGS}
        progress = True
        while progress:
            progress = False
            for e in ENGS:
                lst = self.ops[e]
                while pos[e] < len(lst):
                    it = lst[pos[e]]
                    if it[0] == "w":
                        if sem.get(it[1], 0) < it[2]:
                            break
                    elif it[2] is not None:
                        sem[it[2][0]] = sem.get(it[2][0], 0) + it[2][1]
                    pos[e] += 1
                    progress = True
        for e in ENGS:
            if pos[e] < len(self.ops[e]):
                it = self.ops[e][pos[e]]
                raise RuntimeError("schedule deadlock: engine %s stuck at %d/%d waiting %s >= %s (have %s)" % (
                    e, pos[e], len(self.ops[e]), it[1], it[2], sem.get(it[1], 0)))
        self._simsem = sem

    def emit(self):
        wl = self.waited["sp"]
        for key, cnt in self.dmacnt.items():
            sk = ("d", key)
            if wl.get(sk, 0) < cnt:
                wl[sk] = cnt
                self.ops["sp"].append(("w", sk, cnt))
        self.check()
        nc = self.nc
        ops = self.ops
        self.ops = {e: [] for e in ENGS}
        semmap = self.semmap

        def run(engobj, lst):
            for it in lst:
                if it[0] == "w":
                    engobj.wait_ge(semmap[it[1]], it[2])
                else:
                    ins = it[1](engobj)
                    if it[2] is not None:
                        ins.then_inc(semmap[it[2][0]], it[2][1])

        with nc.Block() as block:
            @block.tensor
            def _(e):
                run(e, ops["pe"])

            @block.scalar
            def _(e):
                run(e, ops["act"])

            @block.vector
            def _(e):
                run(e, ops["dve"])

            @block.gpsimd
            def _(e):
                run(e, ops["pool"])

            @block.sync
            def _(e):
                run(e, ops["sp"])


def MM(out, lhsT, rhs, start=True, stop=True, skip=False):
    if skip:
        return lambda e: e.matmul(out, lhsT, rhs, start=start, stop=stop, skip_group_check=True)
    return lambda e: e.matmul(out, lhsT, rhs, start=start, stop=stop)


def TR(out, in_, ident):
    return lambda e: e.transpose(out, in_, ident)


def ACT(out, in_, func, bias=None, scale=1.0, accum=None):
    kw = {}
    if bias is not None:
        kw["bias"] = bias
    if accum is not None:
        kw["accum_out"] = accum
    return lambda e: e.activation(out=out, in_=in_, func=func, scale=scale, **kw)


def TT(out, a, b, op):
    return lambda e: e.tensor_tensor(out, a, b, op)


def TS(out, a, s1, s2, op0, op1=None):
    if op1 is None:
        return lambda e: e.tensor_scalar(out, a, s1, None, op0)
    return lambda e: e.tensor_scalar(out, a, s1, s2, op0, op1)


def STT(out, in0, scalar, in1, op0, op1):
    return lambda e: e.scalar_tensor_tensor(out, in0, scalar, in1, op0, op1)


def CP(out, in_):
    return lambda e: e.tensor_copy(out, in_)


def DMA(out, in_):
    return lambda e: e.dma_start(out=out, in_=in_)


def MEMSET(ap, v):
    return lambda e: e.memset(ap, v)


def _natten_struct():
    def valid(typ, r, kr):
        segrows = 64 if typ == "P" else 32
        if r // segrows != kr // segrows:
            return False
        base = (r // segrows) * segrows
        rs = min(max((r - base) - 4, 0), segrows - 8) + base
        return rs <= kr < rs + 8

    JL = []
    for m in range(NT):
        js = set()
        for typ in ("P", "S"):
            for j in range(NT):
                if any(valid(typ, 2 * m + f, 2 * j + e) for e in (0, 1) for f in (0, 1)):
                    js.add(j)
        js = sorted(js)
        assert all(abs(j - m) <= 3 for j in js) and len(js) <= 6
        JL.append(js)
    vb = {}
    for typ in ("P", "S"):
        tab = np.zeros((128, NT * 6 * 2), np.float32)
        for m in range(NT):
            for jj, j in enumerate(JL[m]):
                for f in (0, 1):
                    for e in (0, 1):
                        if not valid(typ, 2 * m + f, 2 * j + e):
                            tab[e * 64:(e + 1) * 64, (m * 6 + jj) * 2 + f] = NEG
        vb[typ] = tab
    return JL, vb


def _natten_tt(rpb):
    H = rpb.shape[0]
    c = np.arange(64)
    cs = np.clip(c - 8, 0, 48)
    cp = np.arange(64)
    inwin = (cp[:, None] >= cs[None, :]) & (cp[:, None] < cs[None, :] + 16)
    coff = np.clip(cp[:, None] - c[None, :] + 15, 0, 30)
    G = np.full((H, 17, 64, 64), NEG, np.float32)
    for rho in range(15):
        g = rpb[:, rho][:, coff]
        G[:, rho + 1] = np.where(inwin[None], g, np.float32(NEG))
    tt = np.empty((128, H, 7, 128), np.float32)
    for i0 in range(7):
        rho0 = 2 * i0 + 1
        for e in (0, 1):
            for f in (0, 1):
                rho = rho0 + e - f
                tt[e * 64:(e + 1) * 64, :, i0, f * 64:(f + 1) * 64] = np.transpose(G[:, rho + 1], (1, 0, 2))
    return np.ascontiguousarray(tt.reshape(128, H * 7 * 128))


def _attn_tables(typ):
    a = np.arange(512, dtype=np.float32)[None, :]
    b = np.arange(128, dtype=np.float32)[:, None]
    ab = np.ascontiguousarray(np.broadcast_to(a - b, (128, 512))).astype(np.float32)
    dg = np.stack([np.abs(a - b - 128.0 * j) for j in range(4)], axis=1).astype(np.float32)
    vb2 = np.zeros((128, 8 * 8 * 32), np.float32)
    for h in range(8):
        slope = 2.0 ** (-(h + 1))
        for qb in range(8):
            for kt in range(32):
                delta = 128.0 * (4 * qb - kt)
                if kt < 4 * qb:
                    v = -slope * delta
                elif kt >= 4 * qb + 4:
                    v = slope * delta
                else:
                    v = 0.0
                if typ == "S" and (qb // 4) != (kt // 16):
                    v = NEG
                vb2[:, (h * 8 + qb) * 32 + kt] = v
    return ab, np.ascontiguousarray(dg.reshape(128, 2048)), vb2


import os
DEBUG = bool(int(os.environ.get("KDEBUG", "0")))


def build_program():
    JL, VBT = _natten_struct()
    nc = bass.Bass("TRN2", target_bir_lowering=False)

    def din(name, shape, dt=F32):
        return nc.dram_tensor(name, list(shape), dt, kind="ExternalInput").ap()

    def dscr(name, shape, dt):
        if DEBUG and name in ("xmid", "x1s", "attno", "xmid1"):
            return nc.dram_tensor(name, list(shape), dt, kind="ExternalOutput").ap()
        return nc.dram_tensor(name, list(shape), dt).ap()

    x_in = din("x", [T, D])
    w_in0 = din("l0_w_in", [D, 2560])
    w_out0 = din("l0_w_out", [D, D])
    w_ff1 = [din("l0_w_ff1", [D, DFF]), din("l1_w_ff1", [D, DFF])]
    w_ff2 = [din("l0_w_ff2", [DFF, D]), din("l1_w_ff2", [DFF, D])]
    w_in1 = din("l1_w_in", [D, 3072])
    w_out1 = din("l1_w_out", [D, D])
    gate_gb = din("gate_gb", [128, 1024])
    wsT_in = din("wsT", [128, 512])
    bsT_in = din("bsT", [128, 4])
    tt_in = din("tt", [128, 8 * 7 * 128])
    vb_in = din("vb", [128, NT * 12])
    lnp = [[din("l%d_ln%d" % (l, i), [128, 2048]) for i in (1, 2)] for l in (0, 1)]
    lamv = din("lamv", [128, 256])
    subg_in = din("subg", [128, 128])
    ab_in = din("ab", [128, 512])
    dg_in = din("dg", [128, 2048])
    vb2_in = din("vb2", [128, 2048])
    y_out = nc.dram_tensor("y", [T, D], F32, kind="ExternalOutput").ap()

    w1b = [dscr("w1b%d" % l, [D, DFF], BF16) for l in (0, 1)]
    w2b = [dscr("w2b%d" % l, [DFF, D], BF16) for l in (0, 1)]
    win1b = dscr("win1b", [D, 3072], BF16)
    xmid = dscr("xmid", [T, D], F32)
    xmid1 = dscr("xmid1", [T, D], F32) if DEBUG else xmid
    x1s = dscr("x1s", [T, D], F32)
    qts = dscr("qts", [8, 128, T], BF16)
    attno = dscr("attno", [T, D], BF16)

    with contextlib.ExitStack() as top:
        sems = [top.enter_context(nc.semaphore("s%d" % i)) for i in range(NSEM)]
        S = Sched(nc, sems)
        PA = nc.alloc_psum_tensor("PA", [128, 1024], F32)
        PB = nc.alloc_psum_tensor("PB", [128, 1024], F32)
        PC = nc.alloc_psum_tensor("PC", [128, 512], F32)
        PD = nc.alloc_psum_tensor("PD", [128, 512], F32)
        PE_ = nc.alloc_psum_tensor("PE", [128, 512], F32)
        PT = nc.alloc_psum_tensor("PT", [128, 1024], BF16)
        PAB = [(PA, "PA"), (PB, "PB")]
        P3 = [(PC, "PC"), (PD, "PD"), (PE_, "PE")]
        identf = nc.alloc_sbuf_tensor("identf", [128, 128], F32)
        ident = nc.alloc_sbuf_tensor("ident", [128, 128], BF16)
        i8 = nc.alloc_sbuf_tensor("i8", [128, 128], BF16)
        mhalf = nc.alloc_sbuf_tensor("mhalf", [128, 8], F32)

        S.op("pool", lambda e: e.iota(identf[:], [[1, 128]], base=0, channel_multiplier=-1,
                                      allow_small_or_imprecise_dtypes=True), writes=["identf"])
        S.op("dve", lambda e: e.tensor_single_scalar(ident[:], identf[:], 0.0, ALU.is_equal),
             reads=["identf"], writes=["ident"])
        S.op("dve", TS(i8[:], ident[:], 8.0, None, ALU.mult), reads=["ident"], writes=["i8"])
        S.op("dve", MEMSET(mhalf[:], -0.5), writes=["mhalf"])

        def conv_weight(dst, src, rows, key):
            for r in range(0, rows, 128):
                S.op("pool", DMA(dst[r:r + 128, :], src[r:r + 128, :]), writes=[(key, r // 128)],
                     dma="cv_" + key)

        def layer_norm_tile(yb, dst, gb, kq, ybk, dstk, st):
            bst, mv, ve, rs = st
            for hh in (0, 1):
                S.op("dve", (lambda o, i: (lambda e: e.bn_stats(o, i)))(bst[:, hh * 6:(hh + 1) * 6], yb[:, hh * 512:(hh + 1) * 512]),
                     reads=[ybk], writes=[kq + "bst%d" % hh])
            S.op("dve", (lambda o, i: (lambda e: e.bn_aggr(o, i)))(mv[:, 0:2], bst[:, 0:12]),
                 reads=[kq + "bst0", kq + "bst1"], writes=[kq + "mv"])
            S.op("dve", TS(ve[:, 0:1], mv[:, 1:2], LN_EPS, None, ALU.add), reads=[kq + "mv"], writes=[kq + "ve"])
            S.op("pool", TT(rs[:, 0:1], ve[:, 0:1], mhalf[:, 0:1], ALU.pow), reads=[kq + "ve", "mhalf"], writes=[kq + "rs"])
            S.op("dve", STT(yb[:, :], yb[:, :], mv[:, 0:1], gb[:, 0:1024], ALU.subtract, ALU.mult),
                 reads=[ybk, kq + "mv", "lngb"], writes=[ybk])
            S.op("dve", STT(dst, yb[:, :], rs[:, 0:1], gb[:, 1024:2048], ALU.mult, ALU.add),
                 reads=[ybk, kq + "rs", "lngb"], writes=[dstk])

        def transpose8(src_bf, srck, dst_ap, dstk, n=8, evac="dve", scale=None):
            for c in range(n):
                S.op("pe", TR(PT[:, c * 128:(c + 1) * 128], src_bf[:, c * 128:(c + 1) * 128], ident[:]),
                     reads=[srck, "ident"], writes=["PT"], sig=(c == n - 1))
            src = PT[:, 0:n * 128].rearrange("p (c t) -> p c t", c=n)
            if evac == "act":
                S.op("act", ACT(dst_ap, src, AF.Copy, scale=(1.0 if scale is None else scale)), reads=["PT"], writes=[dstk])
            else:
                S.op("dve", CP(dst_ap, src), reads=["PT"], writes=[dstk])

        def mixer_out(t, catT, catk, wout, woutk, xres, xresk, gb, dst_rows, bufs, pidx):
            yb, xo, st = bufs
            PW, pk = PAB[pidx % 2]
            for half in (0, 1):
                for k in range(8):
                    S.op("pe", MM(PW[:, half * 512:(half + 1) * 512], catT[:, k, :], wout[:, k, half * 512:(half + 1) * 512],
                                  start=(k == 0), stop=(k == 7)),
                         reads=[catk, woutk], writes=[pk], sig=(half == 1 and k == 7))
            ybt, ybk = yb[t % 2], "yb%d" % (t % 2)
            S.op("dve", STT(ybt[:, :], xres, ALPHA, PW[:, :], ALU.mult, ALU.add), reads=[xresk, pk], writes=[ybk])
            xot, xok = xo[t % 2], "xo%d" % (t % 2)
            layer_norm_tile(ybt, xot[:, :], gb, "lnA", ybk, xok, st)
            S.op("sp", DMA(dst_rows, xot[:, :]), reads=[xok], writes=[("xmid", t)], dma="st_xo%d" % (t % 2))

        with contextlib.ExitStack() as ph:
            def sb(name, shape, dt):
                return ph.enter_context(nc.sbuf_tensor(name, list(shape), dt))
            win0 = sb("win0", [128, 8, 2560], BF16)
            wout0 = sb("wout0", [128, 8, 1024], BF16)
            tts = sb("tts", [128, 8, 7, 128], BF16)
            wsT = sb("wsTs", [128, 4, 128], BF16)
            bsT = sb("bsTs", [128, 4], F32)
            ggb = sb("ggb", [128, 1024], F32)
            vb = sb("vbs", [128, NT * 12], F32)
            gb1 = sb("gb1", [128, 2048], F32)
            xr = [sb("xr%d" % i, [128, 1024], F32) for i in range(5)]
            xb = [sb("xb%d" % i, [128, 1024], BF16) for i in range(2)]
            xT = [sb("xT%d" % i, [128, 8, 128], BF16) for i in range(2)]
            qT = [sb("qT%d" % i, [128, 4, 256], BF16) for i in range(8)]
            ntmp = [sb("ntmp%d" % i, [128, 8, 128], F32) for i in range(2)]
            kT = [sb("kT%d" % i, [128, 4, 128], BF16) for i in range(8)]
            va = [sb("va%d" % i, [128, 8, 65], BF16) for i in range(8)]
            g1 = sb("g1", [128, 1024], F32)
            g2 = sb("g2", [128, 1024], F32)
            gz = [sb("gz%d" % i, [128, 1024], F32) for i in range(2)]
            vn = sb("vn", [128, 512], F32)
            vln = [sb("vln%d" % i, [128, 512], BF16) for i in range(2)]
            oa = sb("oa", [128, 512], BF16)
            catT = [sb("catT%d" % i, [128, 8, 128], BF16) for i in range(4)]
            ptb = [sb("ptb%d" % i, [128, 8, 128], BF16) for i in range(2)]
            ob = sb("ob", [128, 512], BF16)
            rec = sb("rec", [128, 8], F32)
            yb = [sb("yb%d" % i, [128, 1024], F32) for i in range(2)]
            xo = [sb("xo%d" % i, [128, 1024], F32) for i in range(2)]
            st1 = (sb("bst1", [128, 12], F32), sb("mv1", [128, 2], F32), sb("ve1", [128, 1], F32), sb("rs1", [128, 1], F32))
            st0 = (sb("bst0", [128, 12], F32), sb("mv0", [128, 2], F32), sb("ve0", [128, 1], F32), sb("rs0", [128, 1], F32))

            w3 = w_in0.rearrange("(k p) c -> p k c", p=128)
            for k in range(8):
                S.op("pool", DMA(win0[:, k, :], w3[:, k, :]), writes=["win0"], dma="ld_win0")
            S.op("pool", DMA(tts[:].rearrange("p h i q -> p (h i q)"), tt_in[:, :]), writes=["tts"], dma="ld_tts")
            S.op("pool", DMA(wsT[:].rearrange("p g t -> p (g t)"), wsT_in[:, :]), writes=["wsT"], dma="ld_wsT")
            w3o = w_out0.rearrange("(k p) c -> p k c", p=128)
            for k in range(8):
                S.op("pool", DMA(wout0[:, k, :], w3o[:, k, :]), writes=["wout0"], dma="ld_wout0")
            S.op("sp", DMA(bsT[:], bsT_in[:, :]), writes=["bsT"], dma="ld_c0")
            S.op("sp", DMA(ggb[:], gate_gb[:, :]), writes=["ggb"], dma="ld_c1")
            S.op("sp", DMA(vb[:], vb_in[:, :]), writes=["vb"], dma="ld_c2")
            S.op("sp", DMA(gb1[:], lnp[0][0][:, :]), writes=["lngb"], dma="ld_c3")
            for i in range(8):
                S.op("dve", MEMSET(va[i][:, :, 64:65], 1.0), writes=["va1_%d" % i])
                S.op("dve", MEMSET(qT[i][:, :, :], 0.0), writes=["qT%d" % i])
            conv_weight(w1b[0], w_ff1[0], D, "w1b0")
            conv_weight(w2b[0], w_ff2[0], DFF, "w2b0")
            conv_weight(win1b, w_in1, D, "win1b")
            conv_weight(w1b[1], w_ff1[1], D, "w1b1")
            conv_weight(w2b[1], w_ff2[1], DFF, "w2b1")

            def x_load(t):
                S.op("sp", DMA(xr[t % 5][:], x_in[t * 128:(t + 1) * 128, :]), writes=["xr%d" % (t % 5)], dma="ld_xr%d" % (t % 5))

            def stage_A1(t):
                xrt, xrk = xr[t % 5], "xr%d" % (t % 5)
                if t == 0:
                    x_load(0)
                if t + 1 < NT:
                    x_load(t + 1)
                xbt, xbk = xb[t % 2], "xb%d" % (t % 2)
                S.op("act", ACT(xbt[:], xrt[:], AF.Copy), reads=[xrk], writes=[xbk])
                xTt, xTk = xT[t % 2], "xT%d" % (t % 2)
                transpose8(xbt, xbk, xTt[:], xTk, evac="act")
                PZ, pzk = PAB[t % 2]
                for half in (0, 1):
                    for k in range(8):
                        S.op("pe", MM(PZ[:, half * 512:(half + 1) * 512], xTt[:, k, :], win0[:, k, half * 512:(half + 1) * 512],
                                      start=(k == 0), stop=(k == 7)),
                             reads=[xTk, "win0"], writes=[pzk], sig=(half == 1 and k == 7))
                gzt, gzk = gz[t % 2], "gz%d" % (t % 2)
                S.op("act", ACT(g1[:], PZ[:, :], AF.Square, scale=math.sqrt(0.044715)), reads=[pzk], writes=["g1"])
                S.op("dve", STT(g2[:], g1[:], 1.0, PZ[:, :], ALU.add, ALU.mult), reads=["g1", pzk], writes=["g2"])
                S.op("act", ACT(g1[:], g2[:], AF.Tanh, scale=math.sqrt(2.0 / math.pi)), reads=["g2"], writes=["g1"])
                S.op("dve", STT(gzt[:], g1[:], 1.0, PZ[:, :], ALU.add, ALU.mult), reads=["g1", pzk], writes=[gzk])
                for which, base, dst, dk, (PQ, pqk) in (("q", 1024, qT[t % 8], "qT%d" % (t % 8), P3[0]), ("k", 1536, kT[t % 8], "kT%d" % (t % 8), P3[1])):
                    for cc in range(4):
                        for k in range(8):
                            S.op("pe", MM(PQ[:, cc * 128:(cc + 1) * 128], win0[:, k, base + cc * 128: base + (cc + 1) * 128], xTt[:, k, :],
                                          start=(k == 0), stop=(k == 7)),
                                 reads=["win0", xTk], writes=[pqk], sig=(cc == 3 and k == 7))
                    if which == "q":
                        S.op("dve", CP(dst[0:64, :, 0:128], PQ[0:64, :].rearrange("p (c t) -> p c t", c=4)), reads=[pqk, dk], writes=[dk + "a"])
                        S.op("dve", CP(dst[64:128, :, 128:256], PQ[64:128, :].rearrange("p (c t) -> p c t", c=4)), reads=[pqk, dk], writes=[dk + "b"])
                    else:
                        S.op("act", ACT(dst[:], PQ[:, :].rearrange("p (c t) -> p c t", c=4), AF.Copy), reads=[pqk], writes=[dk])
                PV_, pvk = P3[2]
                for k in range(8):
                    S.op("pe", MM(PV_[:, :], xTt[:, k, :], win0[:, k, 2048:2560], start=(k == 0), stop=(k == 7)),
                         reads=["win0", xTk], writes=[pvk], sig=(k == 7))
                S.op("act", ACT(va[t % 8][:, :, 0:64], PV_[:, :].rearrange("p (h d) -> p h d", h=8), AF.Copy),
                     reads=[pvk], writes=["va%d" % (t % 8)])
                bst, mv, ve, rs = st0
                S.op("dve", (lambda e: e.bn_stats(bst[:, 0:6], gzt[:, 512:1024])), reads=[gzk], writes=["g_bst"])
                S.op("dve", (lambda e: e.bn_aggr(mv[:, 0:2], bst[:, 0:6])), reads=["g_bst"], writes=["g_mv"])
                S.op("dve", TS(ve[:, 0:1], mv[:, 1:2], 4.0 * LN_EPS, None, ALU.add), reads=["g_mv"], writes=["g_ve"])
                S.op("pool", TT(rs[:, 0:1], ve[:, 0:1], mhalf[:, 0:1], ALU.pow), reads=["g_ve", "mhalf"], writes=["g_rs"])
                S.op("dve", STT(vn[:], gzt[:, 512:1024], mv[:, 0:1], ggb[:, 0:512], ALU.subtract, ALU.mult),
                     reads=[gzk, "g_mv", "ggb"], writes=["vn"])
                S.op("dve", STT(vln[t % 2][:], vn[:], rs[:, 0:1], ggb[:, 512:1024], ALU.mult, ALU.add),
                     reads=["vn", "g_rs", "ggb"], writes=["vln%d" % (t % 2)])

            def stage_A2(t):
                gzt, gzk = gz[t % 2], "gz%d" % (t % 2)
                vl, vlk = vln[t % 2], "vln%d" % (t % 2)
                for g in range(4):
                    S.op("pe", MM(PC[:, g * 128:(g + 1) * 128], wsT[:, g, :], vl[:, g * 128:(g + 1) * 128]),
                         reads=["wsT", vlk], writes=["PC"], sig=(g == 3))
                for g in range(4):
                    S.op("dve", STT(oa[:, g * 128:(g + 1) * 128], PC[:, g * 128:(g + 1) * 128], bsT[:, g:g + 1],
                                    gzt[:, g * 128:(g + 1) * 128], ALU.add, ALU.mult),
                         reads=["PC", "bsT", gzk], writes=["oa"])
                transpose8(oa, "oa", catT[t % 4][:, 0:4, :], "catA%d" % (t % 4), n=4, evac="act", scale=0.5)

            def stage_N(m):
                J = JL[m]
                qk = "qT%d" % (m % 8)

                def nat_S(jj):
                    j = J[jj]
                    PS, psk = PAB[jj % 2]
                    for cc in range(4):
                        S.op("pe", MM(PS[:, cc * 256:(cc + 1) * 256], kT[j % 8][:, cc, :], qT[m % 8][:, cc, :]),
                             reads=["kT%d" % (j % 8), qk, qk + "a", qk + "b"], writes=[psk], sig=(cc == 3))

                def nat_exp(jj):
                    j = J[jj]
                    i0 = j - m + 3
                    PS, psk = PAB[jj % 2]
                    pt, ptk = ptb[jj % 2], "ptb%d" % (jj % 2)
                    nt_, ntk = ntmp[jj % 2], "ntmp%d" % (jj % 2)
                    S.op("dve", STT(nt_[:, :, :], tts[:, :, i0, :], 8.0, PS[:, :].rearrange("p (h q) -> p h q", h=8), ALU.mult, ALU.add),
                         reads=["tts", psk], writes=[ntk])
                    c0 = (m * 6 + jj) * 2
                    same = all(np.array_equal(VBT[typ][:, c0], VBT[typ][:, c0 + 1]) for typ in ("P", "S"))
                    if same:
                        S.op("act", ACT(pt[:, :, :], nt_[:, :, :], AF.Exp, bias=vb[:, c0:c0 + 1], scale=0.125),
                             reads=[ntk, "vb"], writes=[ptk + "f0", ptk + "f1"])
                    else:
                        for f in (0, 1):
                            S.op("act", ACT(pt[:, :, f * 64:(f + 1) * 64], nt_[:, :, f * 64:(f + 1) * 64], AF.Exp,
                                            bias=vb[:, c0 + f:c0 + f + 1], scale=0.125),
                                 reads=[ntk, "vb"], writes=[ptk + "f%d" % f])

                def nat_PV(jj):
                    j = J[jj]
                    pt, ptk = ptb[jj % 2], "ptb%d" % (jj % 2)
                    for h in range(8):
                        PO, pok = (PD, "PD") if h < 4 else (PE_, "PE")
                        c0 = (h % 4) * 65
                        S.op("pe", MM(PO[:, c0:c0 + 65], pt[:, h, :], va[j % 8][:, h, 0:65], start=(jj == 0 and h % 4 == 0), stop=(jj == len(J) - 1), skip=True),
                             reads=[ptk + "f0", ptk + "f1", "va%d" % (j % 8), "va1_%d" % (j % 8)], writes=[pok],
                             sig=(h == 3 or h == 7))

                nat_S(0)
                if len(J) > 1:
                    nat_S(1)
                for jj in range(len(J)):
                    nat_exp(jj)
                    if jj + 2 < len(J):
                        nat_S(jj + 2)
                    nat_PV(jj)
                for half, (PO, pok) in enumerate(((PD, "PD"), (PE_, "PE"))):
                    pov = PO[:, 0:260].rearrange("p (h d) -> p h d", h=4)
                    S.op("dve", (lambda o, i: (lambda e: e.reciprocal(o, i)))(rec[:, half * 4:(half + 1) * 4], pov[:, :, 64]),
                         reads=[pok], writes=["rec%d" % half])
                    S.op("dve", TT(ob[:, half * 256:(half + 1) * 256].rearrange("p (h d) -> p h d", h=4), pov[:, :, 0:64],
                                   rec[:, half * 4:(half + 1) * 4].unsqueeze(2).broadcast_to([128, 4, 64]), ALU.mult),
                         reads=[pok, "rec%d" % half], writes=["ob"])
                transpose8(ob, "ob", catT[m % 4][:, 4:8, :], "catB%d" % (m % 4), n=4, evac="act")

            def _mixer(m):
                ybt, ybk = yb[m % 2], "yb%d" % (m % 2)
                PW, pk = PAB[m % 2]
                ca, cbk = "catA%d" % (m % 4), "catB%d" % (m % 4)
                for half in (0, 1):
                    for k in range(8):
                        S.op("pe", MM(PW[:, half * 512:(half + 1) * 512], catT[m % 4][:, k, :], wout0[:, k, half * 512:(half + 1) * 512],
                                      start=(k == 0), stop=(k == 7)),
                             reads=[ca, cbk, "wout0"], writes=[pk], sig=(half == 1 and k == 7))
                S.op("dve", STT(ybt[:, :], xr[m % 5][:, :], ALPHA, PW[:, :], ALU.mult, ALU.add),
                     reads=["xr%d" % (m % 5), pk], writes=[ybk])
                xot, xok = xo[m % 2], "xo%d" % (m % 2)
                layer_norm_tile(ybt, xot[:, :], gb1, "lnA", ybk, xok, st1)
                S.op("sp", DMA(xmid[m * 128:(m + 1) * 128, :], xot[:, :]), reads=[xok], writes=[("xmid", m)], dma="st_xo%d" % (m % 2))

            for t in range(NT + 3):
                if t < NT:
                    stage_A1(t)
                if t - 3 >= 0:
                    stage_N(t - 3)
                if t < NT:
                    stage_A2(t)
                if t - 3 >= 0:
                    _mixer(t - 3)
            S.emit()

        def ffn_phase(l, src, srckey, dst, dstkey):
            with contextlib.ExitStack() as ph:
                def sb(name, shape, dt):
                    return ph.enter_context(nc.sbuf_tensor("f%d_%s" % (l, name), list(shape), dt))
                w2s = sb("w2s", [128, 32, 1024], BF16)
                gb2 = sb("gb2", [128, 2048], F32)
                xm = [sb("xm%d" % i, [128, 4, 1024], F32) for i in range(2)]
                xb2 = [sb("xb%d" % i, [128, 1024], BF16) for i in range(2)]
                xT = [sb("xT%d" % i, [128, 8, 512], BF16) for i in range(2)]
                w1buf = [sb("w1buf%d" % i, [128, 8, 512], BF16) for i in range(3)]
                hT = sb("hT", [128, 32, 512], BF16)
                rl = [sb("rl%d" % i, [128, 512], F32) for i in range(2)]
                yb = [sb("yb%d" % i, [128, 1024], F32) for i in range(2)]
                xo = [sb("xo%d" % i, [128, 1024], F32) for i in range(2)]
                st = (sb("bst", [128, 12], F32), sb("mv", [128, 2], F32), sb("ve", [128, 1], F32), sb("rs", [128, 1], F32))
                S.op("sp", DMA(gb2[:], lnp[l][1][:, :]), writes=["lngb"], dma="ld_c3")
                w2v = w2b[l].rearrange("(c p) n -> p c n", p=128)
                for c in range(0, 32, 4):
                    S.op("sp", DMA(w2s[:, c:c + 4, :], w2v[:, c:c + 4, :]), reads=[("w2b%d" % l, i) for i in range(32)],
                         writes=["w2s"], dma="ld_w2s")
                w1v = w1b[l].rearrange("(k p) f -> p k f", p=128)
                wcount = [0]
                def blk_load(B):
                    for tl in range(4):
                        t = B * 4 + tl
                        S.op("sp", DMA(xm[B % 2][:, tl, :], src[t * 128:(t + 1) * 128, :]), reads=[(srckey, t)], writes=["xm%d_%d" % (B % 2, tl)],
                             dma="ld_xm%d" % (B % 2))

                def blk_prep(B):
                    xmb, xmk = xm[B % 2], "xm%d" % (B % 2)
                    xTb, xTk = xT[B % 2], "fxT%d" % (B % 2)
                    for tl in range(4):
                        xbt, xbk = xb2[tl % 2], "fxb%d" % (tl % 2)
                        S.op("act", ACT(xbt[:], xmb[:, tl, :], AF.Copy), reads=[xmk + "_%d" % i for i in range(4)], writes=[xbk])
                        transpose8(xbt, xbk, xTb[:, :, tl * 128:(tl + 1) * 128], xTk + "_%d" % tl)

                blk_load(0)
                for B in range(8):
                    xmb, xmk = xm[B % 2], "xm%d" % (B % 2)
                    xTb, xTk = xT[B % 2], "fxT%d" % (B % 2)
                    blk_prep(B)
                    xTkeys = [xTk + "_%d" % i for i in range(4)]
                    for fg in range(8):
                        wi = wcount[0] % 3
                        wcount[0] += 1
                        S.op("sp", DMA(w1buf[wi][:], w1v[:, :, fg * 512:(fg + 1) * 512]),
                             reads=[("w1b%d" % l, i) for i in range(8)], writes=["w1buf%d" % wi], dma="ld_w1buf%d" % wi)
                        for fc in range(4):
                            f = fg * 4 + fc
                            ps, psk = P3[f % 3]
                            for k in range(8):
                                S.op("pe", MM(ps[:, :], w1buf[wi][:, k, fc * 128:(fc + 1) * 128], xTb[:, k, :], start=(k == 0), stop=(k == 7)),
                                     reads=["w1buf%d" % wi] + xTkeys, writes=[psk], sig=(k == 7))
                            r, rk = rl[f % 2], "rl%d" % (f % 2)
                            S.op("act", ACT(r[:], ps[:, :], AF.Relu), reads=[psk], writes=[rk])
                            S.op("dve" if f % 2 == 0 else "pool", TT(hT[:, f, :], r[:], r[:], ALU.mult), reads=[rk], writes=[("hT", f)])
                    hkeys = [("hT", f) for f in range(32)]
                    if B + 1 < 8:
                        blk_load(B + 1)
                    for tl in range(4):
                        t = B * 4 + tl
                        PW, pk = PAB[tl % 2]
                        for half in (0, 1):
                            for f in range(32):
                                S.op("pe", MM(PW[:, half * 512:(half + 1) * 512], hT[:, f, tl * 128:(tl + 1) * 128], w2s[:, f, half * 512:(half + 1) * 512],
                                              start=(f == 0), stop=(f == 31)),
                                     reads=hkeys + ["w2s"] if f == 0 else [], writes=[pk], sig=(half == 1 and f == 31))
                        ybt, ybk = yb[tl % 2], "fyb%d" % (tl % 2)
                        S.op("dve", STT(ybt[:, :], xmb[:, tl, :], ALPHA, PW[:, :], ALU.mult, ALU.add), reads=[xmk + "_%d" % i for i in range(4)] + [pk], writes=[ybk])
                        xot, xok = xo[tl % 2], "fxo%d" % (tl % 2)
                        layer_norm_tile(ybt, xot[:, :], gb2, "lnF", ybk, xok, st)
                        S.op("sp", DMA(dst[t * 128:(t + 1) * 128, :], xot[:, :]), reads=[xok], writes=[(dstkey, t)], dma="st_fxo%d" % (tl % 2))
                S.emit()

        ffn_phase(0, xmid, "xmid", x1s, "x1s")

        with contextlib.ExitStack() as ph2:
            KT = ph2.enter_context(nc.sbuf_tensor("KT", [128, 8, T], BF16))
            VA = ph2.enter_context(nc.sbuf_tensor("VA", [128, 32, 8, 129], BF16))
            S.op("pool", MEMSET(VA[:, :, :, 128:129], 1.0), writes=["VA1"])
            with contextlib.ExitStack() as ph:
                def sb(name, shape, dt):
                    return ph.enter_context(nc.sbuf_tensor("a_" + name, list(shape), dt))
                xm2 = [sb("xm%d" % i, [128, 4, 1024], F32) for i in range(2)]
                xb2 = [sb("xb%d" % i, [128, 1024], BF16) for i in range(2)]
                xT = [sb("xT%d" % i, [128, 8, 512], BF16) for i in range(2)]
                wbuf = [sb("wbuf%d" % i, [128, 8, 512], BF16) for i in range(3)]

                def a_load(B):
                    for tl in range(4):
                        t = B * 4 + tl
                        S.op("sp", DMA(xm2[B % 2][:, tl, :], x1s[t * 128:(t + 1) * 128, :]), reads=[("x1s", t)], writes=["axm%d_%d" % (B % 2, tl)],
                             dma="ld_axm%d" % (B % 2))

                def a_prep(B):
                    xm = xm2[B % 2]
                    for tl in range(4):
                        xbt, xbk = xb2[tl % 2], "axb%d" % (tl % 2)
                        S.op("act", ACT(xbt[:], xm[:, tl, :], AF.Copy), reads=["axm%d_%d" % (B % 2, i) for i in range(4)], writes=[xbk])
                        transpose8(xbt, xbk, xT[B % 2][:, :, tl * 128:(tl + 1) * 128], "axT%d_%d" % (B % 2, tl))

                a_load(0)
                a_prep(0)
                qst = [sb("qst%d" % i, [128, 512], BF16) for i in range(2)]
                wv = win1b.rearrange("(k p) c -> p k c", p=128)
                wc = 0
                qc = 0
                for B in range(8):
                    xTb, xTk = xT[B % 2], "axT%d" % (B % 2)
                    if B + 1 < 8:
                        a_load(B + 1)
                    xTkeys = [xTk + "_%d" % i for i in range(4)]
                    for g in range(6):
                        wi = wc % 3
                        wc += 1
                        S.op("sp", DMA(wbuf[wi][:], wv[:, :, g * 512:(g + 1) * 512]), reads=[("win1b", i) for i in range(8)],
                             writes=["wbuf%d" % wi], dma="ld_wbuf%d" % wi)
                        if g == 4 and B + 1 < 8:
                            a_prep(B + 1)
                        if g < 4:
                            for cc in range(4):
                                h = (g % 2) * 4 + cc
                                ps, psk = P3[(g * 4 + cc) % 3]
                                for k in range(8):
                                    S.op("pe", MM(ps[:, :], wbuf[wi][:, k, cc * 128:(cc + 1) * 128], xTb[:, k, :], start=(k == 0), stop=(k == 7)),
                                         reads=["wbuf%d" % wi] + xTkeys, writes=[psk], sig=(k == 7))
                                if g < 2:
                                    qi = qc % 2
                                    qc += 1
                                    S.op("act", ACT(qst[qi][:], ps[:, :], AF.Copy), reads=[psk], writes=["qst%d" % qi])
                                    S.op("sp", DMA(qts[h, :, B * 512:(B + 1) * 512], qst[qi][:]), reads=["qst%d" % qi], writes=[("qts", h, B)],
                                         dma="st_qst%d" % qi)
                                else:
                                    S.op("dve", CP(KT[:, h, B * 512:(B + 1) * 512], ps[:, :]), reads=[psk], writes=[("KT", h, B)])
                        else:
                            for tl in range(4):
                                t = B * 4 + tl
                                PW, pk = PAB[tl % 2]
                                for k in range(8):
                                    S.op("pe", MM(PW[:, 0:512], xTb[:, k, tl * 128:(tl + 1) * 128], wbuf[wi][:, k, :], start=(k == 0), stop=(k == 7)),
                                         reads=["wbuf%d" % wi] + xTkeys, writes=[pk], sig=(k == 7))
                                eng = "act" if tl % 2 == 0 else "dve"
                                h0 = (g - 4) * 4
                                src = PW[:, 0:512].rearrange("p (h d) -> p h d", h=4)
                                if eng == "act":
                                    S.op("act", ACT(VA[:, t, h0:h0 + 4, 0:128], src, AF.Copy), reads=[pk], writes=[("VA", t, g)])
                                else:
                                    S.op("dve", CP(VA[:, t, h0:h0 + 4, 0:128], src), reads=[pk], writes=[("VA", t, g)])
                S.emit()

            with contextlib.ExitStack() as ph:
                def sb(name, shape, dt):
                    return ph.enter_context(nc.sbuf_tensor("b_" + name, list(shape), dt))
                ab = sb("ab", [128, 512], F32)
                dg = sb("dg", [128, 4, 512], F32)
                vb2 = sb("vb2", [128, 2048], F32)
                lamt = sb("lamt", [128, 256], F32)
                lamp = sb("lamp", [128, 128], F32)
                lsc = sb("lsc", [128, 8], F32)
                subg = sb("subg", [128, 128], F32)
                qtb = [sb("qtb%d" % i, [128, 512], BF16) for i in range(3)]
                tmp = [sb("tmp%d" % i, [128, 1024], F32) for i in range(3)]
                pts = [sb("pts%d" % i, [128, 1024], BF16) for i in range(4)]
                r12 = sb("r12", [128, 8], F32)
                od = sb("od", [128, 4, 128], F32)
                junk = sb("junk", [128, 128], F32)
                ssq = sb("ssq", [128, 4], F32)
                rsq = sb("rsq", [128, 4], F32)
                obf = [sb("obf%d" % i, [128, 4, 128], BF16) for i in range(2)]
                S.op("sp", DMA(ab[:], ab_in[:, :]), writes=["ab"], dma="ld_c0")
                S.op("sp", DMA(dg[:].rearrange("p j a -> p (j a)"), dg_in[:, :]), writes=["dg"], dma="ld_c1")
                S.op("sp", DMA(vb2[:], vb2_in[:, :]), writes=["vb2"], dma="ld_c2")
                S.op("sp", DMA(lamt[:], lamv[:, :]), writes=["lamt"], dma="ld_c3")
                S.op("sp", DMA(subg[:], subg_in[:, :]), writes=["subg"], dma="ld_c4")
                S.op("dve", TT(lamp[:, 0:64], lamt[:, 0:64], lamt[:, 64:128], ALU.mult), reads=["lamt"], writes=["lamp0"])
                S.op("dve", TT(lamp[:, 64:128], lamt[:, 128:192], lamt[:, 192:256], ALU.mult), reads=["lamt"], writes=["lamp1"])
                S.op("dve", (lambda e: e.reduce_sum(lsc[:, 0:2], lamp[:, :].rearrange("p (a b) -> p a b", a=2), mybir.AxisListType.X)),
                     reads=["lamp0", "lamp1"], writes=["lsc01"])
                S.op("act", ACT(lsc[:, 2:4], lsc[:, 0:2], AF.Exp), reads=["lsc01"], writes=["lsc23"])
                S.op("dve", TT(lsc[:, 4:5], lsc[:, 3:4], lsc[:, 2:3], ALU.subtract), reads=["lsc23"], writes=["lsc4"])
                S.op("dve", TS(lsc[:, 4:5], lsc[:, 4:5], -LAM_INIT, None, ALU.add), reads=["lsc4"], writes=["lsc4"])
                S.op("dve", TS(subg[:], subg[:], 1.0 - LAM_INIT, None, ALU.mult), reads=["subg"], writes=["subg"])

                accs = []
                for i in range(8):
                    P, pk = P3[i // 3]
                    c0 = (i % 3) * 129
                    accs.append((P[:, c0:c0 + 129], pk))
                def keep(h, qb, kt):
                    if SKIP_NATS is None:
                        return True
                    if kt < 4 * qb:
                        dmin = 512 * qb - (128 * kt + 127)
                    elif kt >= 4 * qb + 4:
                        dmin = 128 * kt - (512 * qb + 511)
                    else:
                        dmin = 0
                    return (2.0 ** (-(h + 1))) * dmin < SKIP_NATS

                units = [(h, qb, kt) for h in range(8) for qb in range(8) for kt in range(32) if keep(h, qb, kt)]
                NU = len(units)
                first = {}
                last = {}
                for (h_, qb_, kt_) in units:
                    first.setdefault((h_, qb_), kt_)
                    last[(h_, qb_)] = kt_
                qslot = {}

                def q_load(h, qb):
                    i = h * 8 + qb
                    q_t, q_k = qtb[i % 3], "qtb%d" % (i % 3)
                    S.op("sp", DMA(q_t[:], qts[h, :, qb * 512:(qb + 1) * 512]), reads=[("qts", h, qb)], writes=[q_k], dma="ld_" + q_k)

                def s_stage(u):
                    h, qb, kt = units[u]
                    slope = 2.0 ** (-(h + 1))
                    i = h * 8 + qb
                    q_t, q_k = qtb[i % 3], "qtb%d" % (i % 3)
                    if kt == first[(h, qb)] and i + 1 < 64:
                        q_load((i + 1) // 8, (i + 1) % 8)
                    PS, psk = PAB[u % 2]
                    S.op("pe", MM(PS[:, 0:512], KT[0:64, h, kt * 128:(kt + 1) * 128], q_t[0:64, :]),
                         reads=[("KT", h, kt // 4), q_k], writes=[psk], sig=False)
                    S.op("pe", MM(PS[:, 512:1024], KT[64:128, h, kt * 128:(kt + 1) * 128], q_t[64:128, :]),
                         reads=[("KT", h, kt // 4), q_k], writes=[psk], sig=True)
                    if kt < 4 * qb:
                        tab, c, tk = ab[:, :], -8.0 * slope, "ab"
                    elif kt >= 4 * qb + 4:
                        tab, c, tk = ab[:, :], 8.0 * slope, "ab"
                    else:
                        tab, c, tk = dg[:, kt - 4 * qb, :], -8.0 * slope, "dg"
                    tm, tmk = tmp[u % 3], "tmp%d" % (u % 3)
                    S.op("dve", STT(tm[:, :].rearrange("p (m a) -> p m a", m=2), tab.unsqueeze(1).broadcast_to([128, 2, 512]), c,
                                    PS[:, :].rearrange("p (m a) -> p m a", m=2), ALU.mult, ALU.add),
                         reads=[tk, psk], writes=[tmk])
                    col = (h * 8 + qb) * 32 + kt
                    p_t, p_k = pts[u % 4], "pts%d" % (u % 4)
                    S.op("act", ACT(p_t[:, :], tm[:, :], AF.Exp, bias=vb2[:, col:col + 1], scale=0.125),
                         reads=[tmk, "vb2"], writes=[p_k])

                def pv_stage(u):
                    h, qb, kt = units[u]
                    p_t, p_k = pts[u % 4], "pts%d" % (u % 4)
                    for mp in (0, 1):
                        for sbk in range(4):
                            a_ap, a_k = accs[mp * 4 + sbk]
                            S.op("pe", MM(a_ap, p_t[:, mp * 512 + sbk * 128: mp * 512 + (sbk + 1) * 128], VA[:, kt, h, 0:129],
                                          start=(kt == first[(h, qb)] and (mp * 4 + sbk) % 3 == 0), stop=(kt == last[(h, qb)]), skip=True),
                                 reads=[p_k, ("VA", kt, 4 + h // 4), "VA1"], writes=[a_k], sig=(mp == 1 and sbk == 3))

                q_load(0, 0)
                s_stage(0)
                s_stage(1)
                for u in range(NU):
                    h, qb, kt = units[u]
                    if u + 2 < NU:
                        s_stage(u + 2)
                    pv_stage(u)
                    if kt == last[(h, qb)]:
                        for bi, (P, pk) in enumerate(P3):
                            n = 3 if bi < 2 else 2
                            pv = P[:, 0:n * 129].rearrange("p (i d) -> p i d", i=n)
                            S.op("dve", (lambda o, i: (lambda e: e.reciprocal(o, i)))(r12[:, bi * 3: bi * 3 + n], pv[:, :, 128]),
                                 reads=[pk], writes=["r12_%d" % bi])
                        rk = ["r12_0", "r12_1", "r12_2"]
                        S.op("dve", TS(r12[:, 4:8], r12[:, 4:8], lsc[:, 4:5], None, ALU.mult), reads=rk + ["lsc4"], writes=["r12b"])
                        S.op("dve", MEMSET(ssq[:, :], 0.0), writes=["ssq%d" % i for i in range(4)])
                        for sbk in range(4):
                            a1, k1 = accs[sbk]
                            a2, k2 = accs[4 + sbk]
                            S.op("dve", TS(od[:, sbk, :], a1[:, 0:128], r12[:, sbk:sbk + 1], None, ALU.mult), reads=[k1] + rk, writes=["od%d" % sbk])
                            S.op("dve", STT(od[:, sbk, :], a2[:, 0:128], r12[:, 4 + sbk:5 + sbk], od[:, sbk, :], ALU.mult, ALU.add),
                                 reads=[k2, "r12b", "od%d" % sbk], writes=["od%d" % sbk])
                            S.op("act", ACT(junk[:], od[:, sbk, :], AF.Square, accum=ssq[:, sbk:sbk + 1]), reads=["od%d" % sbk],
                                 writes=["junk", "ssq%d" % sbk])
                        sk = ["ssq%d" % i for i in range(4)]
                        S.op("dve", TS(rsq[:, :], ssq[:, :], 1.0 / 128.0, LN_EPS, ALU.mult, ALU.add), reads=sk, writes=["rsq"])
                        S.op("pool", TT(rsq[:, :], rsq[:, :], mhalf[:, 0:4], ALU.pow), reads=["rsq", "mhalf"], writes=["rsq"])
                        o_t, o_k = obf[(h * 8 + qb) % 2], "obf%d" % ((h * 8 + qb) % 2)
                        for sbk in range(4):
                            S.op("dve", STT(o_t[:, sbk, :], od[:, sbk, :], rsq[:, sbk:sbk + 1], subg[:, :], ALU.mult, ALU.mult),
                                 reads=["od%d" % sbk, "rsq", "subg"], writes=[o_k + "_%d" % sbk])
                        dstv = attno[qb * 512:(qb + 1) * 512, h * 128:(h + 1) * 128].rearrange("(s p) c -> p s c", p=128)
                        S.op("sp", DMA(dstv, o_t[:, :, :]), reads=[o_k + "_%d" % i for i in range(4)],
                             writes=[("attno", qb * 4 + i, h) for i in range(4)], dma="st_" + o_k)
                S.emit()

        with contextlib.ExitStack() as ph:
            def sb(name, shape, dt):
                return ph.enter_context(nc.sbuf_tensor("c_" + name, list(shape), dt))
            wout1 = sb("wout1", [128, 8, 1024], BF16)
            gb1 = sb("gb1", [128, 2048], F32)
            ao = [sb("ao%d" % i, [128, 1024], BF16) for i in range(3)]
            oT = [sb("oT%d" % i, [128, 8, 128], BF16) for i in range(2)]
            x1t = [sb("x1t%d" % i, [128, 1024], F32) for i in range(3)]
            yb = [sb("yb%d" % i, [128, 1024], F32) for i in range(2)]
            xo = [sb("xo%d" % i, [128, 1024], F32) for i in range(2)]
            st = (sb("bst", [128, 12], F32), sb("mv", [128, 2], F32), sb("ve", [128, 1], F32), sb("rs", [128, 1], F32))
            w3o = w_out1.rearrange("(k p) c -> p k c", p=128)
            for k in range(8):
                S.op("pool", DMA(wout1[:, k, :], w3o[:, k, :]), writes=["wout1"], dma="ld_wout1")
            S.op("sp", DMA(gb1[:], lnp[1][0][:, :]), writes=["lngb"], dma="ld_c3")
            def c_load(t):
                S.op("sp", DMA(ao[t % 3][:], attno[t * 128:(t + 1) * 128, :]), reads=[("attno", t, hh) for hh in range(8)], writes=["ao%d" % (t % 3)], dma="ld_ao%d" % (t % 3))
                S.op("sp", DMA(x1t[t % 3][:], x1s[t * 128:(t + 1) * 128, :]), reads=[("x1s", t)], writes=["x1t%d" % (t % 3)], dma="ld_x1t%d" % (t % 3))

            c_load(0)
            c_load(1)
            for t in range(NT):
                if t + 2 < NT:
                    c_load(t + 2)
                a_t, a_k = ao[t % 3], "ao%d" % (t % 3)
                x_t, x_k = x1t[t % 3], "x1t%d" % (t % 3)
                transpose8(a_t, a_k, oT[t % 2][:], "oT%d" % (t % 2), evac="act")
                PW, pk = PAB[t % 2]
                for half in (0, 1):
                    for k in range(8):
                        S.op("pe", MM(PW[:, half * 512:(half + 1) * 512], oT[t % 2][:, k, :], wout1[:, k, half * 512:(half + 1) * 512],
                                      start=(k == 0), stop=(k == 7)),
                             reads=["oT%d" % (t % 2), "wout1"], writes=[pk], sig=(half == 1 and k == 7))
                ybt, ybk = yb[t % 2], "cyb%d" % (t % 2)
                S.op("dve", STT(ybt[:, :], x_t[:, :], ALPHA, PW[:, :], ALU.mult, ALU.add), reads=[x_k, pk], writes=[ybk])
                xot, xok = xo[t % 2], "cxo%d" % (t % 2)
                layer_norm_tile(ybt, xot[:, :], gb1, "lnC", ybk, xok, st)
                S.op("sp", DMA(xmid1[t * 128:(t + 1) * 128, :], xot[:, :]), reads=[xok], writes=[("xmid", t)], dma="st_cxo%d" % (t % 2))
            S.emit()

        ffn_phase(1, xmid1, "xmid", y_out, "y")
        S.fence("sp", [("y", t) for t in range(NT)])
        S.emit()
    return nc


_CACHE = {}


def kernel(x_prompt, x_sample, l0_w_in, l0_w_out, l0_gate_ln_g, l0_gate_ln_b, l0_w_spatial, l0_b_spatial, l0_na_rpb,
           l0_ln1_g, l0_ln1_b, l0_w_ff1, l0_w_ff2, l0_ln2_g, l0_ln2_b, l1_w_in, l1_w_out, l1_lambda_q1, l1_lambda_k1,
           l1_lambda_q2, l1_lambda_k2, l1_subln_g, l1_ln1_g, l1_ln1_b, l1_w_ff1, l1_w_ff2, l1_ln2_g, l1_ln2_b):
    f = lambda a: np.ascontiguousarray(np.asarray(a, dtype=np.float32))
    x_prompt, x_sample = f(x_prompt), f(x_sample)
    if "nc" not in _CACHE:
        _CACHE["nc"] = build_program()
    nc = _CACHE["nc"]
    _, vbt = _natten_struct()

    def rep(*vs):
        v = np.concatenate([f(a).reshape(-1) for a in vs])
        return np.ascontiguousarray(np.broadcast_to(v[None, :], (128, v.size)))

    shared = {
        "l0_w_in": f(l0_w_in), "l0_w_out": f(l0_w_out), "l0_w_ff1": f(l0_w_ff1), "l0_w_ff2": f(l0_w_ff2),
        "l1_w_in": f(l1_w_in), "l1_w_out": f(l1_w_out), "l1_w_ff1": f(l1_w_ff1), "l1_w_ff2": f(l1_w_ff2),
        "gate_gb": rep(l0_gate_ln_g, l0_gate_ln_b),
        "wsT": np.ascontiguousarray(np.transpose(f(l0_w_spatial), (2, 0, 1)).reshape(128, 512)),
        "bsT": np.ascontiguousarray(f(l0_b_spatial).T),
        "tt": _natten_tt(f(l0_na_rpb)),
        "l0_ln1": rep(l0_ln1_g, l0_ln1_b), "l0_ln2": rep(l0_ln2_g, l0_ln2_b),
        "l1_ln1": rep(l1_ln1_g, l1_ln1_b), "l1_ln2": rep(l1_ln2_g, l1_ln2_b),
        "lamv": rep(l1_lambda_q1, l1_lambda_k1, l1_lambda_q2, l1_lambda_k2),
        "subg": rep(l1_subln_g),
    }
    tabs = {typ: _attn_tables(typ) for typ in ("P", "S")}
    in_maps = []
    for c in range(8):
        typ = "P" if c < 4 else "S"
        xc = x_prompt[c] if c < 4 else x_sample[2 * (c - 4):2 * (c - 4) + 2].reshape(T, D)
        m = dict(shared)
        m["x"] = np.ascontiguousarray(xc)
        m["vb"] = vbt[typ]
        m["ab"], m["dg"], m["vb2"] = tabs[typ]
        in_maps.append(m)
    res = run_bass_kernel_spmd(nc, in_maps, core_ids=list(range(8)))
    if DEBUG:
        _CACHE["dbg"] = res.results
    ys = [np.asarray(r["y"], dtype=np.float32) for r in res.results]
    y_prompt = np.stack(ys[0:4], axis=0)
    y_sample = np.stack(ys[4:8], axis=0).reshape(8, 2048, D)
    return (y_prompt, y_sample)
```

```python
import contextlib
import math
import numpy as np
import concourse.bass as bass
import concourse.mybir as mybir
from concourse.bass_utils import run_bass_kernel_spmd

F32 = mybir.dt.float32
BF16 = mybir.dt.bfloat16
AF = mybir.ActivationFunctionType
ALU = mybir.AluOpType

D = 1024
T = 4096
NT = 32
DFF = 4096
ALPHA = 4.0 ** 0.25
LAM_INIT = 0.8 - 0.6 * math.exp(-0.3)
LN_EPS = 1e-5
NEG = -30000.0
ENGS = ("pe", "act", "dve", "pool", "sp")
EPOCH = 30000
NSEM = 80
SKIP_NATS = 100.0


class Sched:
    def __init__(self, nc, sems):
        self.nc = nc
        self.sempool = sems
        self.semmap = {}
        self.ops = {e: [] for e in ENGS}
        self.cnt = {e: 0 for e in ENGS}
        self.waited = {e: {} for e in ENGS}
        self.lastw = {}
        self.readers = {}
        self.dmacnt = {}
        self.ninst = 0

    def _sem(self, key):
        if key not in self.semmap:
            self.semmap[key] = self.sempool[len(self.semmap)]
        return key

    def op(self, eng, fn, reads=(), writes=(), sig=True, dma=None):
        deps = []
        for r in reads:
            t = self.lastw.get(r)
            if t is not None:
                deps.append(t)
        for w in writes:
            t = self.lastw.get(w)
            if t is not None:
                deps.append(t)
            rd = self.readers.get(w)
            if rd:
                deps.extend(rd.values())
        wl = self.waited[eng]
        need = {}
        for (sk, val, teng, isdma) in deps:
            if teng == eng and eng == "pe" and not isdma:
                continue
            if isdma:
                val = self.dmacnt[sk[1]]
            if need.get(sk, 0) < val:
                need[sk] = val
        for sk, val in need.items():
            if wl.get(sk, 0) < val:
                wl[sk] = val
                self.ops[eng].append(("w", sk, val))
        if dma is not None:
            c = self.dmacnt.get(dma, 0) + 16
            self.dmacnt[dma] = c
            tok = (self._sem(("d", dma)), c, eng, True)
            inc = (tok[0], 16)
        else:
            n = self.cnt[eng] + 1
            if sig:
                self.cnt[eng] = n
            idx = n - 1
            tok = (self._sem(("e", eng, idx // EPOCH)), idx % EPOCH + 1, eng, False)
            inc = (tok[0], 1) if sig else None
        self.ops[eng].append(("o", fn, inc))
        self.ninst += 1
        for w in writes:
            self.lastw[w] = tok
            self.readers[w] = {}
        for r in reads:
            d = self.readers.setdefault(r, {})
            k = tok[0]
            if k not in d or d[k][1] < tok[1]:
                d[k] = tok

    def fence(self, eng, keys):
        wl = self.waited[eng]
        for k in keys:
            t = self.lastw.get(k)
            if t is None:
                continue
            if wl.get(t[0], 0) < t[1]:
                wl[t[0]] = t[1]
                self.ops[eng].append(("w", t[0], t[1]))

    def check(self):
        sem = getattr(self, "_simsem", {})
        pos = {e: 0 for e in ENGS}
        progress = True
        while progress:
            progress = False
            for e in ENGS:
                lst = self.ops[e]
                while pos[e] < len(lst):
                    it = lst[pos[e]]
                    if it[0] == "w":
                        if sem.get(it[1], 0) < it[2]:
                            break
                    elif it[2] is not None:
                        sem[it[2][0]] = sem.get(it[2][0], 0) + it[2][1]
                    pos[e] += 1
                    progress = True
        for e in ENGS:
            if pos[e] < len(self.ops[e]):
                it = self.ops[e][pos[e]]
                raise RuntimeError("schedule deadlock: engine %s stuck at %d/%d waiting %s >= %s (have %s)" % (
                    e, pos[e], len(self.ops[e]), it[1], it[2], sem.get(it[1], 0)))
        self._simsem = sem

    def emit(self):
        wl = self.waited["sp"]
        for key, cnt in self.dmacnt.items():
            sk = ("d", key)
            if wl.get(sk, 0) < cnt:
                wl[sk] = cnt
                self.ops["sp"].append(("w", sk, cnt))
        self.check()
        nc = self.nc
        ops = self.ops
        self.ops = {e: [] for e in ENGS}
        semmap = self.semmap

        def run(engobj, lst):
            for it in lst:
                if it[0] == "w":
                    engobj.wait_ge(semmap[it[1]], it[2])
                else:
                    ins = it[1](engobj)
                    if it[2] is not None:
                        ins.then_inc(semmap[it[2][0]], it[2][1])

        with nc.Block() as block:
            @block.tensor
            def _(e):
                run(e, ops["pe"])

            @block.scalar
            def _(e):
                run(e, ops["act"])

            @block.vector
            def _(e):
                run(e, ops["dve"])

            @block.gpsimd
            def _(e):
                run(e, ops["pool"])

            @block.sync
            def _(e):
                run(e, ops["sp"])


def MM(out, lhsT, rhs, start=True, stop=True, skip=False):
    if skip:
        return lambda e: e.matmul(out, lhsT, rhs, start=start, stop=stop, skip_group_check=True)
    return lambda e: e.matmul(out, lhsT, rhs, start=start, stop=stop)


def TR(out, in_, ident):
    return lambda e: e.transpose(out, in_, ident)


def ACT(out, in_, func, bias=None, scale=1.0, accum=None):
    kw = {}
    if bias is not None:
        kw["bias"] = bias
    if accum is not None:
        kw["accum_out"] = accum
    return lambda e: e.activation(out=out, in_=in_, func=func, scale=scale, **kw)


def TT(out, a, b, op):
    return lambda e: e.tensor_tensor(out, a, b, op)


def TS(out, a, s1, s2, op0, op1=None):
    if op1 is None:
        return lambda e: e.tensor_scalar(out, a, s1, None, op0)
    return lambda e: e.tensor_scalar(out, a, s1, s2, op0, op1)


def STT(out, in0, scalar, in1, op0, op1):
    return lambda e: e.scalar_tensor_tensor(out, in0, scalar, in1, op0, op1)


def CP(out, in_):
    return lambda e: e.tensor_copy(out, in_)


def DMA(out, in_):
    return lambda e: e.dma_start(out=out, in_=in_)


def MEMSET(ap, v):
    return lambda e: e.memset(ap, v)


def _natten_struct():
    def valid(typ, r, kr):
        segrows = 64 if typ == "P" else 32
        if r // segrows != kr // segrows:
            return False
        base = (r // segrows) * segrows
        rs = min(max((r - base) - 4, 0), segrows - 8) + base
        return rs <= kr < rs + 8

    JL = []
    for m in range(NT):
        js = set()
        for typ in ("P", "S"):
            for j in range(NT):
                if any(valid(typ, 2 * m + f, 2 * j + e) for e in (0, 1) for f in (0, 1)):
                    js.add(j)
        js = sorted(js)
        assert all(abs(j - m) <= 3 for j in js) and len(js) <= 6
        JL.append(js)
    vb = {}
    for typ in ("P", "S"):
        tab = np.zeros((128, NT * 6 * 2), np.float32)
        for m in range(NT):
            for jj, j in enumerate(JL[m]):
                for f in (0, 1):
                    for e in (0, 1):
                        if not valid(typ, 2 * m + f, 2 * j + e):
                            tab[e * 64:(e + 1) * 64, (m * 6 + jj) * 2 + f] = NEG
        vb[typ] = tab
    return JL, vb


def _natten_tt(rpb):
    H = rpb.shape[0]
    c = np.arange(64)
    cs = np.clip(c - 8, 0, 48)
    cp = np.arange(64)
    inwin = (cp[:, None] >= cs[None, :]) & (cp[:, None] < cs[None, :] + 16)
    coff = np.clip(cp[:, None] - c[None, :] + 15, 0, 30)
    G = np.full((H, 17, 64, 64), NEG, np.float32)
    for rho in range(15):
        g = rpb[:, rho][:, coff]
        G[:, rho + 1] = np.where(inwin[None], g, np.float32(NEG))
    tt = np.empty((128, H, 7, 128), np.float32)
    for i0 in range(7):
        rho0 = 2 * i0 + 1
        for e in (0, 1):
            for f in (0, 1):
                rho = rho0 + e - f
                tt[e * 64:(e + 1) * 64, :, i0, f * 64:(f + 1) * 64] = np.transpose(G[:, rho + 1], (1, 0, 2))
    return np.ascontiguousarray(tt.reshape(128, H * 7 * 128))


def _attn_tables(typ):
    a = np.arange(512, dtype=np.float32)[None, :]
    b = np.arange(128, dtype=np.float32)[:, None]
    arow = np.ascontiguousarray(np.broadcast_to(np.concatenate([a, 511.0 - a], axis=1), (128, 1024))).astype(np.float32)
    dg = np.stack([np.abs(a - b - 128.0 * j) for j in range(4)], axis=1).astype(np.float32)
    vb2 = np.zeros((128, 8 * 8 * 32), np.float32)
    p = np.arange(128, dtype=np.float32)
    for h in range(8):
        slope = 2.0 ** (-(h + 1))
        for qb in range(8):
            for kt in range(32):
                delta = 128.0 * (4 * qb - kt)
                if kt < 4 * qb:
                    v = -slope * (delta - p)
                elif kt >= 4 * qb + 4:
                    v = -slope * (-delta - 511.0 + p)
                else:
                    v = np.zeros(128, np.float32)
                if typ == "S" and (qb // 4) != (kt // 16):
                    v = np.full(128, NEG, np.float32)
                vb2[:, (h * 8 + qb) * 32 + kt] = v
    return arow, np.ascontiguousarray(dg.reshape(128, 2048)), vb2


import os
DEBUG = bool(int(os.environ.get("KDEBUG", "0")))


def build_program():
    JL, VBT = _natten_struct()
    nc = bass.Bass("TRN2", target_bir_lowering=False)

    def din(name, shape, dt=F32):
        return nc.dram_tensor(name, list(shape), dt, kind="ExternalInput").ap()

    def dscr(name, shape, dt):
        if DEBUG and name in ("xmid", "x1s", "attno", "xmid1"):
            return nc.dram_tensor(name, list(shape), dt, kind="ExternalOutput").ap()
        return nc.dram_tensor(name, list(shape), dt).ap()

    x_in = din("x", [T, D])
    w_in0 = din("l0_w_in", [D, 2560])
    w_out0 = din("l0_w_out", [D, D])
    w_ff1 = [din("l0_w_ff1", [D, DFF]), din("l1_w_ff1", [D, DFF])]
    w_ff2 = [din("l0_w_ff2", [DFF, D]), din("l1_w_ff2", [DFF, D])]
    w_in1 = din("l1_w_in", [D, 3072])
    w_out1 = din("l1_w_out", [D, D])
    gate_gb = din("gate_gb", [128, 1024])
    wsT_in = din("wsT", [128, 512])
    bsT_in = din("bsT", [128, 4])
    tt_in = din("tt", [128, 8 * 7 * 128])
    vb_in = din("vb", [128, NT * 12])
    lnp = [[din("l%d_ln%d" % (l, i), [128, 2048]) for i in (1, 2)] for l in (0, 1)]
    lamv = din("lamv", [128, 256])
    subg_in = din("subg", [128, 128])
    ab_in = din("ab", [128, 1024])
    dg_in = din("dg", [128, 2048])
    vb2_in = din("vb2", [128, 2048])
    y_out = nc.dram_tensor("y", [T, D], F32, kind="ExternalOutput").ap()

    w1b = [dscr("w1b%d" % l, [D, DFF], BF16) for l in (0, 1)]
    w2b = [dscr("w2b%d" % l, [DFF, D], BF16) for l in (0, 1)]
    win1b = dscr("win1b", [D, 3072], BF16)
    xmid = dscr("xmid", [T, D], F32)
    xmid1 = dscr("xmid1", [T, D], F32) if DEBUG else xmid
    x1s = dscr("x1s", [T, D], F32)
    qts = dscr("qts", [8, 128, T], BF16)
    attno = dscr("attno", [T, D], BF16)

    with contextlib.ExitStack() as top:
        sems = [top.enter_context(nc.semaphore("s%d" % i)) for i in range(NSEM)]
        S = Sched(nc, sems)
        PA = nc.alloc_psum_tensor("PA", [128, 1024], F32)
        PB = nc.alloc_psum_tensor("PB", [128, 1024], F32)
        PC = nc.alloc_psum_tensor("PC", [128, 512], F32)
        PD = nc.alloc_psum_tensor("PD", [128, 512], F32)
        PE_ = nc.alloc_psum_tensor("PE", [128, 512], F32)
        PT = nc.alloc_psum_tensor("PT", [128, 1024], BF16)
        PAB = [(PA, "PA"), (PB, "PB")]
        P3 = [(PC, "PC"), (PD, "PD"), (PE_, "PE")]
        identf = nc.alloc_sbuf_tensor("identf", [128, 128], F32)
        ident = nc.alloc_sbuf_tensor("ident", [128, 128], BF16)
        i8 = nc.alloc_sbuf_tensor("i8", [128, 128], BF16)
        mhalf = nc.alloc_sbuf_tensor("mhalf", [128, 8], F32)

        S.op("pool", lambda e: e.iota(identf[:], [[1, 128]], base=0, channel_multiplier=-1,
                                      allow_small_or_imprecise_dtypes=True), writes=["identf"])
        S.op("dve", lambda e: e.tensor_single_scalar(ident[:], identf[:], 0.0, ALU.is_equal),
             reads=["identf"], writes=["ident"])
        S.op("dve", TS(i8[:], ident[:], 8.0, None, ALU.mult), reads=["ident"], writes=["i8"])
        S.op("dve", MEMSET(mhalf[:], -0.5), writes=["mhalf"])

        def conv_weight(dst, src, rows, key):
            for r in range(0, rows, 128):
                S.op("pool", DMA(dst[r:r + 128, :], src[r:r + 128, :]), writes=[(key, r // 128)],
                     dma="cv_" + key)

        def layer_norm_tile(yb, dst, gb, kq, ybk, dstk, st):
            bst, mv, ve, rs = st
            for hh in (0, 1):
                S.op("dve", (lambda o, i: (lambda e: e.bn_stats(o, i)))(bst[:, hh * 6:(hh + 1) * 6], yb[:, hh * 512:(hh + 1) * 512]),
                     reads=[ybk], writes=[kq + "bst%d" % hh])
            S.op("dve", (lambda o, i: (lambda e: e.bn_aggr(o, i)))(mv[:, 0:2], bst[:, 0:12]),
                 reads=[kq + "bst0", kq + "bst1"], writes=[kq + "mv"])
            S.op("dve", TS(ve[:, 0:1], mv[:, 1:2], LN_EPS, None, ALU.add), reads=[kq + "mv"], writes=[kq + "ve"])
            S.op("pool", TT(rs[:, 0:1], ve[:, 0:1], mhalf[:, 0:1], ALU.pow), reads=[kq + "ve", "mhalf"], writes=[kq + "rs"])
            S.op("dve", STT(yb[:, :], yb[:, :], mv[:, 0:1], gb[:, 0:1024], ALU.subtract, ALU.mult),
                 reads=[ybk, kq + "mv", "lngb"], writes=[ybk])
            S.op("dve", STT(dst, yb[:, :], rs[:, 0:1], gb[:, 1024:2048], ALU.mult, ALU.add),
                 reads=[ybk, kq + "rs", "lngb"], writes=[dstk])

        def transpose8(src_bf, srck, dst_ap, dstk, n=8, evac="dve", scale=None):
            for c in range(n):
                S.op("pe", TR(PT[:, c * 128:(c + 1) * 128], src_bf[:, c * 128:(c + 1) * 128], ident[:]),
                     reads=[srck, "ident"], writes=["PT"], sig=(c == n - 1))
            src = PT[:, 0:n * 128].rearrange("p (c t) -> p c t", c=n)
            if evac == "act":
                S.op("act", ACT(dst_ap, src, AF.Copy, scale=(1.0 if scale is None else scale)), reads=["PT"], writes=[dstk])
            else:
                S.op("dve", CP(dst_ap, src), reads=["PT"], writes=[dstk])

        def mixer_out(t, catT, catk, wout, woutk, xres, xresk, gb, dst_rows, bufs, pidx):
            yb, xo, st = bufs
            PW, pk = PAB[pidx % 2]
            for half in (0, 1):
                for k in range(8):
                    S.op("pe", MM(PW[:, half * 512:(half + 1) * 512], catT[:, k, :], wout[:, k, half * 512:(half + 1) * 512],
                                  start=(k == 0), stop=(k == 7)),
                         reads=[catk, woutk], writes=[pk], sig=(half == 1 and k == 7))
            ybt, ybk = yb[t % 2], "yb%d" % (t % 2)
            S.op("dve", STT(ybt[:, :], xres, ALPHA, PW[:, :], ALU.mult, ALU.add), reads=[xresk, pk], writes=[ybk])
            xot, xok = xo[t % 2], "xo%d" % (t % 2)
            layer_norm_tile(ybt, xot[:, :], gb, "lnA", ybk, xok, st)
            S.op("sp", DMA(dst_rows, xot[:, :]), reads=[xok], writes=[("xmid", t)], dma="st_xo%d" % (t % 2))

        with contextlib.ExitStack() as ph:
            def sb(name, shape, dt):
                return ph.enter_context(nc.sbuf_tensor(name, list(shape), dt))
            win0 = sb("win0", [128, 8, 2560], BF16)
            wout0 = sb("wout0", [128, 8, 1024], BF16)
            tts = sb("tts", [128, 8, 7, 128], BF16)
            wsT = sb("wsTs", [128, 4, 128], BF16)
            bsT = sb("bsTs", [128, 4], F32)
            ggb = sb("ggb", [128, 1024], F32)
            vb = sb("vbs", [128, NT * 12], F32)
            gb1 = sb("gb1", [128, 2048], F32)
            xr = [sb("xr%d" % i, [128, 1024], F32) for i in range(5)]
            xb = [sb("xb%d" % i, [128, 1024], BF16) for i in range(2)]
            xT = [sb("xT%d" % i, [128, 8, 128], BF16) for i in range(2)]
            qT = [sb("qT%d" % i, [128, 4, 256], BF16) for i in range(8)]
            ntmp = [sb("ntmp%d" % i, [128, 8, 128], F32) for i in range(2)]
            kT = [sb("kT%d" % i, [128, 4, 128], BF16) for i in range(8)]
            va = [sb("va%d" % i, [128, 8, 65], BF16) for i in range(8)]
            g1 = sb("g1", [128, 1024], F32)
            g2 = sb("g2", [128, 1024], F32)
            gz = [sb("gz%d" % i, [128, 1024], F32) for i in range(2)]
            vn = sb("vn", [128, 512], F32)
            vln = [sb("vln%d" % i, [128, 512], BF16) for i in range(2)]
            oa = sb("oa", [128, 512], BF16)
            catT = [sb("catT%d" % i, [128, 8, 128], BF16) for i in range(4)]
            ptb = [sb("ptb%d" % i, [128, 8, 128], BF16) for i in range(2)]
            ob = sb("ob", [128, 512], BF16)
            rec = sb("rec", [128, 8], F32)
            yb = [sb("yb%d" % i, [128, 1024], F32) for i in range(2)]
            xo = [sb("xo%d" % i, [128, 1024], F32) for i in range(2)]
            st1 = (sb("bst1", [128, 12], F32), sb("mv1", [128, 2], F32), sb("ve1", [128, 1], F32), sb("rs1", [128, 1], F32))
            st0 = (sb("bst0", [128, 12], F32), sb("mv0", [128, 2], F32), sb("ve0", [128, 1], F32), sb("rs0", [128, 1], F32))

            w3 = w_in0.rearrange("(k p) c -> p k c", p=128)
            for k in range(8):
                S.op("pool", DMA(win0[:, k, :], w3[:, k, :]), writes=["win0"], dma="ld_win0")
            S.op("pool", DMA(tts[:].rearrange("p h i q -> p (h i q)"), tt_in[:, :]), writes=["tts"], dma="ld_tts")
            S.op("pool", DMA(wsT[:].rearrange("p g t -> p (g t)"), wsT_in[:, :]), writes=["wsT"], dma="ld_wsT")
            w3o = w_out0.rearrange("(k p) c -> p k c", p=128)
            for k in range(8):
                S.op("pool", DMA(wout0[:, k, :], w3o[:, k, :]), writes=["wout0"], dma="ld_wout0")
            S.op("sp", DMA(bsT[:], bsT_in[:, :]), writes=["bsT"], dma="ld_c0")
            S.op("sp", DMA(ggb[:], gate_gb[:, :]), writes=["ggb"], dma="ld_c1")
            S.op("sp", DMA(vb[:], vb_in[:, :]), writes=["vb"], dma="ld_c2")
            S.op("sp", DMA(gb1[:], lnp[0][0][:, :]), writes=["lngb"], dma="ld_c3")
            for i in range(8):
                S.op("dve", MEMSET(va[i][:, :, 64:65], 1.0), writes=["va1_%d" % i])
                S.op("dve", MEMSET(qT[i][:, :, :], 0.0), writes=["qT%d" % i])
            conv_weight(w1b[0], w_ff1[0], D, "w1b0")
            conv_weight(w2b[0], w_ff2[0], DFF, "w2b0")
            conv_weight(win1b, w_in1, D, "win1b")
            conv_weight(w1b[1], w_ff1[1], D, "w1b1")
            conv_weight(w2b[1], w_ff2[1], DFF, "w2b1")

            def x_load(t):
                S.op("sp", DMA(xr[t % 5][:], x_in[t * 128:(t + 1) * 128, :]), writes=["xr%d" % (t % 5)], dma="ld_xr%d" % (t % 5))

            def stage_A1(t):
                xrt, xrk = xr[t % 5], "xr%d" % (t % 5)
                if t == 0:
                    x_load(0)
                if t + 1 < NT:
                    x_load(t + 1)
                xbt, xbk = xb[t % 2], "xb%d" % (t % 2)
                S.op("act", ACT(xbt[:], xrt[:], AF.Copy), reads=[xrk], writes=[xbk])
                xTt, xTk = xT[t % 2], "xT%d" % (t % 2)
                transpose8(xbt, xbk, xTt[:], xTk, evac="act")
                PZ, pzk = PAB[t % 2]
                for half in (0, 1):
                    for k in range(8):
                        S.op("pe", MM(PZ[:, half * 512:(half + 1) * 512], xTt[:, k, :], win0[:, k, half * 512:(half + 1) * 512],
                                      start=(k == 0), stop=(k == 7)),
                             reads=[xTk, "win0"], writes=[pzk], sig=(half == 1 and k == 7))
                gzt, gzk = gz[t % 2], "gz%d" % (t % 2)
                S.op("act", ACT(g1[:], PZ[:, :], AF.Square, scale=math.sqrt(0.044715)), reads=[pzk], writes=["g1"])
                S.op("dve", STT(g2[:], g1[:], 1.0, PZ[:, :], ALU.add, ALU.mult), reads=["g1", pzk], writes=["g2"])
                S.op("act", ACT(g1[:], g2[:], AF.Tanh, scale=math.sqrt(2.0 / math.pi)), reads=["g2"], writes=["g1"])
                S.op("dve", STT(gzt[:], g1[:], 1.0, PZ[:, :], ALU.add, ALU.mult), reads=["g1", pzk], writes=[gzk])
                for which, base, dst, dk, (PQ, pqk) in (("q", 1024, qT[t % 8], "qT%d" % (t % 8), P3[0]), ("k", 1536, kT[t % 8], "kT%d" % (t % 8), P3[1])):
                    for cc in range(4):
                        for k in range(8):
                            S.op("pe", MM(PQ[:, cc * 128:(cc + 1) * 128], win0[:, k, base + cc * 128: base + (cc + 1) * 128], xTt[:, k, :],
                                          start=(k == 0), stop=(k == 7)),
                                 reads=["win0", xTk], writes=[pqk], sig=(cc == 3 and k == 7))
                    if which == "q":
                        S.op("dve", CP(dst[0:64, :, 0:128], PQ[0:64, :].rearrange("p (c t) -> p c t", c=4)), reads=[pqk, dk], writes=[dk + "a"])
                        S.op("dve", CP(dst[64:128, :, 128:256], PQ[64:128, :].rearrange("p (c t) -> p c t", c=4)), reads=[pqk, dk], writes=[dk + "b"])
                    else:
                        S.op("act", ACT(dst[:], PQ[:, :].rearrange("p (c t) -> p c t", c=4), AF.Copy), reads=[pqk], writes=[dk])
                PV_, pvk = P3[2]
                for k in range(8):
                    S.op("pe", MM(PV_[:, :], xTt[:, k, :], win0[:, k, 2048:2560], start=(k == 0), stop=(k == 7)),
                         reads=["win0", xTk], writes=[pvk], sig=(k == 7))
                S.op("act", ACT(va[t % 8][:, :, 0:64], PV_[:, :].rearrange("p (h d) -> p h d", h=8), AF.Copy),
                     reads=[pvk], writes=["va%d" % (t % 8)])
                bst, mv, ve, rs = st0
                S.op("dve", (lambda e: e.bn_stats(bst[:, 0:6], gzt[:, 512:1024])), reads=[gzk], writes=["g_bst"])
                S.op("dve", (lambda e: e.bn_aggr(mv[:, 0:2], bst[:, 0:6])), reads=["g_bst"], writes=["g_mv"])
                S.op("dve", TS(ve[:, 0:1], mv[:, 1:2], 4.0 * LN_EPS, None, ALU.add), reads=["g_mv"], writes=["g_ve"])
                S.op("pool", TT(rs[:, 0:1], ve[:, 0:1], mhalf[:, 0:1], ALU.pow), reads=["g_ve", "mhalf"], writes=["g_rs"])
                S.op("dve", STT(vn[:], gzt[:, 512:1024], mv[:, 0:1], ggb[:, 0:512], ALU.subtract, ALU.mult),
                     reads=[gzk, "g_mv", "ggb"], writes=["vn"])
                S.op("dve", STT(vln[t % 2][:], vn[:], rs[:, 0:1], ggb[:, 512:1024], ALU.mult, ALU.add),
                     reads=["vn", "g_rs", "ggb"], writes=["vln%d" % (t % 2)])

            def stage_A2(t):
                gzt, gzk = gz[t % 2], "gz%d" % (t % 2)
                vl, vlk = vln[t % 2], "vln%d" % (t % 2)
                for g in range(4):
                    S.op("pe", MM(PC[:, g * 128:(g + 1) * 128], wsT[:, g, :], vl[:, g * 128:(g + 1) * 128]),
                         reads=["wsT", vlk], writes=["PC"], sig=(g == 3))
                for g in range(4):
                    S.op("dve", STT(oa[:, g * 128:(g + 1) * 128], PC[:, g * 128:(g + 1) * 128], bsT[:, g:g + 1],
                                    gzt[:, g * 128:(g + 1) * 128], ALU.add, ALU.mult),
                         reads=["PC", "bsT", gzk], writes=["oa"])
                transpose8(oa, "oa", catT[t % 4][:, 0:4, :], "catA%d" % (t % 4), n=4, evac="act", scale=0.5)

            def stage_N(m):
                J = JL[m]
                qk = "qT%d" % (m % 8)

                def nat_S(jj):
                    j = J[jj]
                    PS, psk = PAB[jj % 2]
                    for cc in range(4):
                        S.op("pe", MM(PS[:, cc * 256:(cc + 1) * 256], kT[j % 8][:, cc, :], qT[m % 8][:, cc, :]),
                             reads=["kT%d" % (j % 8), qk, qk + "a", qk + "b"], writes=[psk], sig=(cc == 3))

                def nat_exp(jj):
                    j = J[jj]
                    i0 = j - m + 3
                    PS, psk = PAB[jj % 2]
                    pt, ptk = ptb[jj % 2], "ptb%d" % (jj % 2)
                    nt_, ntk = ntmp[jj % 2], "ntmp%d" % (jj % 2)
                    S.op("dve", STT(nt_[:, :, :], tts[:, :, i0, :], 8.0, PS[:, :].rearrange("p (h q) -> p h q", h=8), ALU.mult, ALU.add),
                         reads=["tts", psk], writes=[ntk])
                    c0 = (m * 6 + jj) * 2
                    same = all(np.array_equal(VBT[typ][:, c0], VBT[typ][:, c0 + 1]) for typ in ("P", "S"))
                    if same:
                        S.op("act", ACT(pt[:, :, :], nt_[:, :, :], AF.Exp, bias=vb[:, c0:c0 + 1], scale=0.125),
                             reads=[ntk, "vb"], writes=[ptk + "f0", ptk + "f1"])
                    else:
                        for f in (0, 1):
                            S.op("act", ACT(pt[:, :, f * 64:(f + 1) * 64], nt_[:, :, f * 64:(f + 1) * 64], AF.Exp,
                                            bias=vb[:, c0 + f:c0 + f + 1], scale=0.125),
                                 reads=[ntk, "vb"], writes=[ptk + "f%d" % f])

                def nat_PV(jj):
                    j = J[jj]
                    pt, ptk = ptb[jj % 2], "ptb%d" % (jj % 2)
                    for h in range(8):
                        PO, pok = (PD, "PD") if h < 4 else (PE_, "PE")
                        c0 = (h % 4) * 65
                        S.op("pe", MM(PO[:, c0:c0 + 65], pt[:, h, :], va[j % 8][:, h, 0:65], start=(jj == 0 and h % 4 == 0), stop=(jj == len(J) - 1), skip=True),
                             reads=[ptk + "f0", ptk + "f1", "va%d" % (j % 8), "va1_%d" % (j % 8)], writes=[pok],
                             sig=(h == 3 or h == 7))

                nat_S(0)
                if len(J) > 1:
                    nat_S(1)
                for jj in range(len(J)):
                    nat_exp(jj)
                    if jj + 2 < len(J):
                        nat_S(jj + 2)
                    nat_PV(jj)
                for half, (PO, pok) in enumerate(((PD, "PD"), (PE_, "PE"))):
                    pov = PO[:, 0:260].rearrange("p (h d) -> p h d", h=4)
                    S.op("dve", (lambda o, i: (lambda e: e.reciprocal(o, i)))(rec[:, half * 4:(half + 1) * 4], pov[:, :, 64]),
                         reads=[pok], writes=["rec%d" % half])
                    S.op("dve", TT(ob[:, half * 256:(half + 1) * 256].rearrange("p (h d) -> p h d", h=4), pov[:, :, 0:64],
                                   rec[:, half * 4:(half + 1) * 4].unsqueeze(2).broadcast_to([128, 4, 64]), ALU.mult),
                         reads=[pok, "rec%d" % half], writes=["ob"])
                transpose8(ob, "ob", catT[m % 4][:, 4:8, :], "catB%d" % (m % 4), n=4, evac="act")

            def _mixer(m):
                ybt, ybk = yb[m % 2], "yb%d" % (m % 2)
                PW, pk = PAB[m % 2]
                ca, cbk = "catA%d" % (m % 4), "catB%d" % (m % 4)
                for half in (0, 1):
                    for k in range(8):
                        S.op("pe", MM(PW[:, half * 512:(half + 1) * 512], catT[m % 4][:, k, :], wout0[:, k, half * 512:(half + 1) * 512],
                                      start=(k == 0), stop=(k == 7)),
                             reads=[ca, cbk, "wout0"], writes=[pk], sig=(half == 1 and k == 7))
                S.op("dve", STT(ybt[:, :], xr[m % 5][:, :], ALPHA, PW[:, :], ALU.mult, ALU.add),
                     reads=["xr%d" % (m % 5), pk], writes=[ybk])
                xot, xok = xo[m % 2], "xo%d" % (m % 2)
                layer_norm_tile(ybt, xot[:, :], gb1, "lnA", ybk, xok, st1)
                S.op("sp", DMA(xmid[m * 128:(m + 1) * 128, :], xot[:, :]), reads=[xok], writes=[("xmid", m)], dma="st_xo%d" % (m % 2))

            for t in range(NT + 3):
                if t < NT:
                    stage_A1(t)
                if t - 3 >= 0:
                    stage_N(t - 3)
                if t < NT:
                    stage_A2(t)
                if t - 3 >= 0:
                    _mixer(t - 3)
            S.emit()

        def ffn_phase(l, src, srckey, dst, dstkey):
            with contextlib.ExitStack() as ph:
                def sb(name, shape, dt):
                    return ph.enter_context(nc.sbuf_tensor("f%d_%s" % (l, name), list(shape), dt))
                w2s = sb("w2s", [128, 32, 1024], BF16)
                gb2 = sb("gb2", [128, 2048], F32)
                xm = [sb("xm%d" % i, [128, 4, 1024], F32) for i in range(2)]
                xb2 = [sb("xb%d" % i, [128, 1024], BF16) for i in range(2)]
                xT = [sb("xT%d" % i, [128, 8, 512], BF16) for i in range(2)]
                w1buf = [sb("w1buf%d" % i, [128, 8, 512], BF16) for i in range(3)]
                hT = sb("hT", [128, 32, 512], BF16)
                rl = [sb("rl%d" % i, [128, 512], F32) for i in range(2)]
                yb = [sb("yb%d" % i, [128, 1024], F32) for i in range(2)]
                xo = [sb("xo%d" % i, [128, 1024], F32) for i in range(2)]
                st = (sb("bst", [128, 12], F32), sb("mv", [128, 2], F32), sb("ve", [128, 1], F32), sb("rs", [128, 1], F32))
                S.op("sp", DMA(gb2[:], lnp[l][1][:, :]), writes=["lngb"], dma="ld_c3")
                w2v = w2b[l].rearrange("(c p) n -> p c n", p=128)
                for c in range(0, 32, 4):
                    S.op("sp", DMA(w2s[:, c:c + 4, :], w2v[:, c:c + 4, :]), reads=[("w2b%d" % l, i) for i in range(32)],
                         writes=["w2s"], dma="ld_w2s")
                w1v = w1b[l].rearrange("(k p) f -> p k f", p=128)
                wcount = [0]
                def blk_load(B):
                    for tl in range(4):
                        t = B * 4 + tl
                        S.op("sp", DMA(xm[B % 2][:, tl, :], src[t * 128:(t + 1) * 128, :]), reads=[(srckey, t)], writes=["xm%d_%d" % (B % 2, tl)],
                             dma="ld_xm%d" % (B % 2))

                def blk_prep(B):
                    xmb, xmk = xm[B % 2], "xm%d" % (B % 2)
                    xTb, xTk = xT[B % 2], "fxT%d" % (B % 2)
                    for tl in range(4):
                        xbt, xbk = xb2[tl % 2], "fxb%d" % (tl % 2)
                        S.op("act", ACT(xbt[:], xmb[:, tl, :], AF.Copy), reads=[xmk + "_%d" % i for i in range(4)], writes=[xbk])
                        transpose8(xbt, xbk, xTb[:, :, tl * 128:(tl + 1) * 128], xTk + "_%d" % tl)

                blk_load(0)
                for B in range(8):
                    xmb, xmk = xm[B % 2], "xm%d" % (B % 2)
                    xTb, xTk = xT[B % 2], "fxT%d" % (B % 2)
                    blk_prep(B)
                    xTkeys = [xTk + "_%d" % i for i in range(4)]
                    for fg in range(8):
                        wi = wcount[0] % 3
                        wcount[0] += 1
                        S.op("sp", DMA(w1buf[wi][:], w1v[:, :, fg * 512:(fg + 1) * 512]),
                             reads=[("w1b%d" % l, i) for i in range(8)], writes=["w1buf%d" % wi], dma="ld_w1buf%d" % wi)
                        for fc in range(4):
                            f = fg * 4 + fc
                            ps, psk = P3[f % 3]
                            for k in range(8):
                                S.op("pe", MM(ps[:, :], w1buf[wi][:, k, fc * 128:(fc + 1) * 128], xTb[:, k, :], start=(k == 0), stop=(k == 7)),
                                     reads=["w1buf%d" % wi] + xTkeys, writes=[psk], sig=(k == 7))
                            r, rk = rl[f % 2], "rl%d" % (f % 2)
                            S.op("act", ACT(r[:], ps[:, :], AF.Relu), reads=[psk], writes=[rk])
                            S.op("dve" if f % 2 == 0 else "pool", TT(hT[:, f, :], r[:], r[:], ALU.mult), reads=[rk], writes=[("hT", f)])
                    hkeys = [("hT", f) for f in range(32)]
                    if B + 1 < 8:
                        blk_load(B + 1)
                    for tl in range(4):
                        t = B * 4 + tl
                        PW, pk = PAB[tl % 2]
                        for half in (0, 1):
                            for f in range(32):
                                S.op("pe", MM(PW[:, half * 512:(half + 1) * 512], hT[:, f, tl * 128:(tl + 1) * 128], w2s[:, f, half * 512:(half + 1) * 512],
                                              start=(f == 0), stop=(f == 31)),
                                     reads=hkeys + ["w2s"] if f == 0 else [], writes=[pk], sig=(half == 1 and f == 31))
                        ybt, ybk = yb[tl % 2], "fyb%d" % (tl % 2)
                        S.op("dve", STT(ybt[:, :], xmb[:, tl, :], ALPHA, PW[:, :], ALU.mult, ALU.add), reads=[xmk + "_%d" % i for i in range(4)] + [pk], writes=[ybk])
                        xot, xok = xo[tl % 2], "fxo%d" % (tl % 2)
                        layer_norm_tile(ybt, xot[:, :], gb2, "lnF", ybk, xok, st)
                        S.op("sp", DMA(dst[t * 128:(t + 1) * 128, :], xot[:, :]), reads=[xok], writes=[(dstkey, t)], dma="st_fxo%d" % (tl % 2))
                S.emit()

        ffn_phase(0, xmid, "xmid", x1s, "x1s")

        with contextlib.ExitStack() as ph2:
            KT = ph2.enter_context(nc.sbuf_tensor("KT", [128, 8, T], BF16))
            VA = ph2.enter_context(nc.sbuf_tensor("VA", [128, 32, 8, 129], BF16))
            S.op("pool", MEMSET(VA[:, :, :, 128:129], 1.0), writes=["VA1"])
            with contextlib.ExitStack() as ph:
                def sb(name, shape, dt):
                    return ph.enter_context(nc.sbuf_tensor("a_" + name, list(shape), dt))
                xm2 = [sb("xm%d" % i, [128, 4, 1024], F32) for i in range(2)]
                xb2 = [sb("xb%d" % i, [128, 1024], BF16) for i in range(2)]
                xT = [sb("xT%d" % i, [128, 8, 512], BF16) for i in range(2)]
                wbuf = [sb("wbuf%d" % i, [128, 8, 512], BF16) for i in range(3)]

                def a_load(B):
                    for tl in range(4):
                        t = B * 4 + tl
                        S.op("sp", DMA(xm2[B % 2][:, tl, :], x1s[t * 128:(t + 1) * 128, :]), reads=[("x1s", t)], writes=["axm%d_%d" % (B % 2, tl)],
                             dma="ld_axm%d" % (B % 2))

                def a_prep(B):
                    xm = xm2[B % 2]
                    for tl in range(4):
                        xbt, xbk = xb2[tl % 2], "axb%d" % (tl % 2)
                        S.op("act", ACT(xbt[:], xm[:, tl, :], AF.Copy), reads=["axm%d_%d" % (B % 2, i) for i in range(4)], writes=[xbk])
                        transpose8(xbt, xbk, xT[B % 2][:, :, tl * 128:(tl + 1) * 128], "axT%d_%d" % (B % 2, tl))

                a_load(0)
                a_prep(0)
                qst = [sb("qst%d" % i, [128, 512], BF16) for i in range(2)]
                wv = win1b.rearrange("(k p) c -> p k c", p=128)
                wc = 0
                qc = 0
                for B in range(8):
                    xTb, xTk = xT[B % 2], "axT%d" % (B % 2)
                    if B + 1 < 8:
                        a_load(B + 1)
                    xTkeys = [xTk + "_%d" % i for i in range(4)]
                    for g in range(6):
                        wi = wc % 3
                        wc += 1
                        S.op("sp", DMA(wbuf[wi][:], wv[:, :, g * 512:(g + 1) * 512]), reads=[("win1b", i) for i in range(8)],
                             writes=["wbuf%d" % wi], dma="ld_wbuf%d" % wi)
                        if g == 4 and B + 1 < 8:
                            a_prep(B + 1)
                        if g < 4:
                            for cc in range(4):
                                h = (g % 2) * 4 + cc
                                ps, psk = P3[(g * 4 + cc) % 3]
                                for k in range(8):
                                    S.op("pe", MM(ps[:, :], wbuf[wi][:, k, cc * 128:(cc + 1) * 128], xTb[:, k, :], start=(k == 0), stop=(k == 7)),
                                         reads=["wbuf%d" % wi] + xTkeys, writes=[psk], sig=(k == 7))
                                if g < 2:
                                    qi = qc % 2
                                    qc += 1
                                    S.op("act", ACT(qst[qi][:], ps[:, :], AF.Copy), reads=[psk], writes=["qst%d" % qi])
                                    S.op("sp", DMA(qts[h, :, B * 512:(B + 1) * 512], qst[qi][:]), reads=["qst%d" % qi], writes=[("qts", h, B)],
                                         dma="st_qst%d" % qi)
                                else:
                                    S.op("dve", CP(KT[:, h, B * 512:(B + 1) * 512], ps[:, :]), reads=[psk], writes=[("KT", h, B)])
                        else:
                            for tl in range(4):
                                t = B * 4 + tl
                                PW, pk = PAB[tl % 2]
                                for k in range(8):
                                    S.op("pe", MM(PW[:, 0:512], xTb[:, k, tl * 128:(tl + 1) * 128], wbuf[wi][:, k, :], start=(k == 0), stop=(k == 7)),
                                         reads=["wbuf%d" % wi] + xTkeys, writes=[pk], sig=(k == 7))
                                eng = "act" if tl % 2 == 0 else "dve"
                                h0 = (g - 4) * 4
                                src = PW[:, 0:512].rearrange("p (h d) -> p h d", h=4)
                                if eng == "act":
                                    S.op("act", ACT(VA[:, t, h0:h0 + 4, 0:128], src, AF.Copy), reads=[pk], writes=[("VA", t, g)])
                                else:
                                    S.op("dve", CP(VA[:, t, h0:h0 + 4, 0:128], src), reads=[pk], writes=[("VA", t, g)])
                S.emit()

            with contextlib.ExitStack() as ph:
                def sb(name, shape, dt):
                    return ph.enter_context(nc.sbuf_tensor("b_" + name, list(shape), dt))
                ab = sb("ab", [128, 1024], F32)
                EL = sb("EL", [128, 2, 512], BF16)
                ER = sb("ER", [128, 2, 512], BF16)
                pre = [sb("pre%d" % i, [128, 1024], BF16) for i in range(3)]
                dg = sb("dg", [128, 4, 512], F32)
                vb2 = sb("vb2", [128, 2048], F32)
                lamt = sb("lamt", [128, 256], F32)
                lamp = sb("lamp", [128, 128], F32)
                lsc = sb("lsc", [128, 8], F32)
                subg = sb("subg", [128, 128], F32)
                qtb = [sb("qtb%d" % i, [128, 512], BF16) for i in range(3)]
                tmp = [sb("tmp%d" % i, [128, 1024], F32) for i in range(3)]
                pts = [sb("pts%d" % i, [128, 1024], BF16) for i in range(4)]
                r12 = sb("r12", [128, 8], F32)
                od = sb("od", [128, 4, 128], F32)
                junk = sb("junk", [128, 128], F32)
                ssq = sb("ssq", [128, 4], F32)
                rsq = sb("rsq", [128, 4], F32)
                obf = [sb("obf%d" % i, [128, 4, 128], BF16) for i in range(2)]
                S.op("sp", DMA(ab[:], ab_in[:, :]), writes=["ab"], dma="ld_c0")
                S.op("sp", DMA(dg[:].rearrange("p j a -> p (j a)"), dg_in[:, :]), writes=["dg"], dma="ld_c1")
                S.op("sp", DMA(vb2[:], vb2_in[:, :]), writes=["vb2"], dma="ld_c2")
                S.op("sp", DMA(lamt[:], lamv[:, :]), writes=["lamt"], dma="ld_c3")
                S.op("sp", DMA(subg[:], subg_in[:, :]), writes=["subg"], dma="ld_c4")
                S.op("dve", TT(lamp[:, 0:64], lamt[:, 0:64], lamt[:, 64:128], ALU.mult), reads=["lamt"], writes=["lamp0"])
                S.op("dve", TT(lamp[:, 64:128], lamt[:, 128:192], lamt[:, 192:256], ALU.mult), reads=["lamt"], writes=["lamp1"])
                S.op("dve", (lambda e: e.reduce_sum(lsc[:, 0:2], lamp[:, :].rearrange("p (a b) -> p a b", a=2), mybir.AxisListType.X)),
                     reads=["lamp0", "lamp1"], writes=["lsc01"])
                S.op("act", ACT(lsc[:, 2:4], lsc[:, 0:2], AF.Exp), reads=["lsc01"], writes=["lsc23"])
                S.op("dve", TT(lsc[:, 4:5], lsc[:, 3:4], lsc[:, 2:3], ALU.subtract), reads=["lsc23"], writes=["lsc4"])
                S.op("dve", TS(lsc[:, 4:5], lsc[:, 4:5], -LAM_INIT, None, ALU.add), reads=["lsc4"], writes=["lsc4"])
                S.op("dve", TS(subg[:], subg[:], 1.0 - LAM_INIT, None, ALU.mult), reads=["subg"], writes=["subg"])

                accs = []
                for i in range(8):
                    P, pk = P3[i // 3]
                    c0 = (i % 3) * 129
                    accs.append((P[:, c0:c0 + 129], pk))
                def keep(h, qb, kt):
                    if SKIP_NATS is None:
                        return True
                    if kt < 4 * qb:
                        dmin = 512 * qb - (128 * kt + 127)
                    elif kt >= 4 * qb + 4:
                        dmin = 128 * kt - (512 * qb + 511)
                    else:
                        dmin = 0
                    return (2.0 ** (-(h + 1))) * dmin < SKIP_NATS

                units = [(h, qb, kt) for h in range(8) for qb in range(8) for kt in range(32) if keep(h, qb, kt)]
                NU = len(units)
                first = {}
                last = {}
                for (h_, qb_, kt_) in units:
                    first.setdefault((h_, qb_), kt_)
                    last[(h_, qb_)] = kt_
                qslot = {}

                def q_load(h, qb):
                    i = h * 8 + qb
                    q_t, q_k = qtb[i % 3], "qtb%d" % (i % 3)
                    S.op("sp", DMA(q_t[:], qts[h, :, qb * 512:(qb + 1) * 512]), reads=[("qts", h, qb)], writes=[q_k], dma="ld_" + q_k)

                def s_stage(u):
                    h, qb, kt = units[u]
                    slope = 2.0 ** (-(h + 1))
                    i = h * 8 + qb
                    q_t, q_k = qtb[i % 3], "qtb%d" % (i % 3)
                    if kt == first[(h, qb)] and i + 1 < 64:
                        q_load((i + 1) // 8, (i + 1) % 8)
                    PS, psk = PAB[u % 2]
                    S.op("pe", MM(PS[:, 0:512], KT[0:64, h, kt * 128:(kt + 1) * 128], q_t[0:64, :]),
                         reads=[("KT", h, kt // 4), q_k], writes=[psk], sig=False)
                    S.op("pe", MM(PS[:, 512:1024], KT[64:128, h, kt * 128:(kt + 1) * 128], q_t[64:128, :]),
                         reads=[("KT", h, kt // 4), q_k], writes=[psk], sig=True)
                    col = (h * 8 + qb) * 32 + kt
                    p_t, p_k = pts[u % 4], "pts%d" % (u % 4)
                    if qb == 0 and kt == 0:
                        for mm in (0, 1):
                            S.op("act", ACT(EL[:, mm, :], ab[:, 0:512], AF.Exp, scale=-slope), reads=["ab"], writes=["EL"])
                            S.op("act", ACT(ER[:, mm, :], ab[:, 512:1024], AF.Exp, scale=-slope), reads=["ab"], writes=["ER"])
                    if 4 * qb <= kt < 4 * qb + 4:
                        tm, tmk = tmp[u % 3], "tmp%d" % (u % 3)
                        S.op("dve", STT(tm[:, :].rearrange("p (m a) -> p m a", m=2), dg[:, kt - 4 * qb, :].unsqueeze(1).broadcast_to([128, 2, 512]),
                                        -8.0 * slope, PS[:, :].rearrange("p (m a) -> p m a", m=2), ALU.mult, ALU.add),
                             reads=["dg", psk], writes=[tmk])
                        S.op("act", ACT(p_t[:, :], tm[:, :], AF.Exp, bias=vb2[:, col:col + 1], scale=0.125),
                             reads=[tmk, "vb2"], writes=[p_k])
                    else:
                        E, ek = (EL, "EL") if kt < 4 * qb else (ER, "ER")
                        pr, prk = pre[u % 3], "pre%d" % (u % 3)
                        S.op("act", ACT(pr[:, :], PS[:, :], AF.Exp, bias=vb2[:, col:col + 1], scale=0.125),
                             reads=[psk, "vb2"], writes=[prk])
                        S.op("dve", TT(p_t[:, :], pr[:, :], E[:, :, :].rearrange("p m a -> p (m a)"), ALU.mult),
                             reads=[prk, ek], writes=[p_k])

                def pv_stage(u):
                    h, qb, kt = units[u]
                    p_t, p_k = pts[u % 4], "pts%d" % (u % 4)
                    for mp in (0, 1):
                        for sbk in range(4):
                            a_ap, a_k = accs[mp * 4 + sbk]
                            S.op("pe", MM(a_ap, p_t[:, mp * 512 + sbk * 128: mp * 512 + (sbk + 1) * 128], VA[:, kt, h, 0:129],
                                          start=(kt == first[(h, qb)] and (mp * 4 + sbk) % 3 == 0), stop=(kt == last[(h, qb)]), skip=True),
                                 reads=[p_k, ("VA", kt, 4 + h // 4), "VA1"], writes=[a_k], sig=(mp == 1 and sbk == 3))

                q_load(0, 0)
                s_stage(0)
                s_stage(1)
                for u in range(NU):
                    h, qb, kt = units[u]
                    if u + 2 < NU:
                        s_stage(u + 2)
                    pv_stage(u)
                    if kt == last[(h, qb)]:
                        for bi, (P, pk) in enumerate(P3):
                            n = 3 if bi < 2 else 2
                            pv = P[:, 0:n * 129].rearrange("p (i d) -> p i d", i=n)
                            S.op("dve", (lambda o, i: (lambda e: e.reciprocal(o, i)))(r12[:, bi * 3: bi * 3 + n], pv[:, :, 128]),
                                 reads=[pk], writes=["r12_%d" % bi])
                        rk = ["r12_0", "r12_1", "r12_2"]
                        S.op("dve", TS(r12[:, 4:8], r12[:, 4:8], lsc[:, 4:5], None, ALU.mult), reads=rk + ["lsc4"], writes=["r12b"])
                        S.op("dve", MEMSET(ssq[:, :], 0.0), writes=["ssq%d" % i for i in range(4)])
                        for sbk in range(4):
                            a1, k1 = accs[sbk]
                            a2, k2 = accs[4 + sbk]
                            S.op("dve", TS(od[:, sbk, :], a1[:, 0:128], r12[:, sbk:sbk + 1], None, ALU.mult), reads=[k1] + rk, writes=["od%d" % sbk])
                            S.op("dve", STT(od[:, sbk, :], a2[:, 0:128], r12[:, 4 + sbk:5 + sbk], od[:, sbk, :], ALU.mult, ALU.add),
                                 reads=[k2, "r12b", "od%d" % sbk], writes=["od%d" % sbk])
                            S.op("act", ACT(junk[:], od[:, sbk, :], AF.Square, accum=ssq[:, sbk:sbk + 1]), reads=["od%d" % sbk],
                                 writes=["junk", "ssq%d" % sbk])
                        sk = ["ssq%d" % i for i in range(4)]
                        S.op("dve", TS(rsq[:, :], ssq[:, :], 1.0 / 128.0, LN_EPS, ALU.mult, ALU.add), reads=sk, writes=["rsq"])
                        S.op("pool", TT(rsq[:, :], rsq[:, :], mhalf[:, 0:4], ALU.pow), reads=["rsq", "mhalf"], writes=["rsq"])
                        o_t, o_k = obf[(h * 8 + qb) % 2], "obf%d" % ((h * 8 + qb) % 2)
                        for sbk in range(4):
                            S.op("dve", STT(o_t[:, sbk, :], od[:, sbk, :], rsq[:, sbk:sbk + 1], subg[:, :], ALU.mult, ALU.mult),
                                 reads=["od%d" % sbk, "rsq", "subg"], writes=[o_k + "_%d" % sbk])
                        dstv = attno[qb * 512:(qb + 1) * 512, h * 128:(h + 1) * 128].rearrange("(s p) c -> p s c", p=128)
                        S.op("sp", DMA(dstv, o_t[:, :, :]), reads=[o_k + "_%d" % i for i in range(4)],
                             writes=[("attno", qb * 4 + i, h) for i in range(4)], dma="st_" + o_k)
                S.emit()

        with contextlib.ExitStack() as ph:
            def sb(name, shape, dt):
                return ph.enter_context(nc.sbuf_tensor("c_" + name, list(shape), dt))
            wout1 = sb("wout1", [128, 8, 1024], BF16)
            gb1 = sb("gb1", [128, 2048], F32)
            ao = [sb("ao%d" % i, [128, 1024], BF16) for i in range(3)]
            oT = [sb("oT%d" % i, [128, 8, 128], BF16) for i in range(2)]
            x1t = [sb("x1t%d" % i, [128, 1024], F32) for i in range(3)]
            yb = [sb("yb%d" % i, [128, 1024], F32) for i in range(2)]
            xo = [sb("xo%d" % i, [128, 1024], F32) for i in range(2)]
            st = (sb("bst", [128, 12], F32), sb("mv", [128, 2], F32), sb("ve", [128, 1], F32), sb("rs", [128, 1], F32))
            w3o = w_out1.rearrange("(k p) c -> p k c", p=128)
            for k in range(8):
                S.op("pool", DMA(wout1[:, k, :], w3o[:, k, :]), writes=["wout1"], dma="ld_wout1")
            S.op("sp", DMA(gb1[:], lnp[1][0][:, :]), writes=["lngb"], dma="ld_c3")
            def c_load(t):
                S.op("sp", DMA(ao[t % 3][:], attno[t * 128:(t + 1) * 128, :]), reads=[("attno", t, hh) for hh in range(8)], writes=["ao%d" % (t % 3)], dma="ld_ao%d" % (t % 3))
                S.op("sp", DMA(x1t[t % 3][:], x1s[t * 128:(t + 1) * 128, :]), reads=[("x1s", t)], writes=["x1t%d" % (t % 3)], dma="ld_x1t%d" % (t % 3))

            c_load(0)
            c_load(1)
            for t in range(NT):
                if t + 2 < NT:
                    c_load(t + 2)
                a_t, a_k = ao[t % 3], "ao%d" % (t % 3)
                x_t, x_k = x1t[t % 3], "x1t%d" % (t % 3)
                transpose8(a_t, a_k, oT[t % 2][:], "oT%d" % (t % 2), evac="act")
                PW, pk = PAB[t % 2]
                for half in (0, 1):
                    for k in range(8):
                        S.op("pe", MM(PW[:, half * 512:(half + 1) * 512], oT[t % 2][:, k, :], wout1[:, k, half * 512:(half + 1) * 512],
                                      start=(k == 0), stop=(k == 7)),
                             reads=["oT%d" % (t % 2), "wout1"], writes=[pk], sig=(half == 1 and k == 7))
                ybt, ybk = yb[t % 2], "cyb%d" % (t % 2)
                S.op("dve", STT(ybt[:, :], x_t[:, :], ALPHA, PW[:, :], ALU.mult, ALU.add), reads=[x_k, pk], writes=[ybk])
                xot, xok = xo[t % 2], "cxo%d" % (t % 2)
                layer_norm_tile(ybt, xot[:, :], gb1, "lnC", ybk, xok, st)
                S.op("sp", DMA(xmid1[t * 128:(t + 1) * 128, :], xot[:, :]), reads=[xok], writes=[("xmid", t)], dma="st_cxo%d" % (t % 2))
            S.emit()

        ffn_phase(1, xmid1, "xmid", y_out, "y")
        S.fence("sp", [("y", t) for t in range(NT)])
        S.emit()
    return nc


_CACHE = {}


def kernel(x_prompt, x_sample, l0_w_in, l0_w_out, l0_gate_ln_g, l0_gate_ln_b, l0_w_spatial, l0_b_spatial, l0_na_rpb,
           l0_ln1_g, l0_ln1_b, l0_w_ff1, l0_w_ff2, l0_ln2_g, l0_ln2_b, l1_w_in, l1_w_out, l1_lambda_q1, l1_lambda_k1,
           l1_lambda_q2, l1_lambda_k2, l1_subln_g, l1_ln1_g, l1_ln1_b, l1_w_ff1, l1_w_ff2, l1_ln2_g, l1_ln2_b):
    f = lambda a: np.ascontiguousarray(np.asarray(a, dtype=np.float32))
    x_prompt, x_sample = f(x_prompt), f(x_sample)
    if "nc" not in _CACHE:
        _CACHE["nc"] = build_program()
    nc = _CACHE["nc"]
    _, vbt = _natten_struct()

    def rep(*vs):
        v = np.concatenate([f(a).reshape(-1) for a in vs])
        return np.ascontiguousarray(np.broadcast_to(v[None, :], (128, v.size)))

    shared = {
        "l0_w_in": f(l0_w_in), "l0_w_out": f(l0_w_out), "l0_w_ff1": f(l0_w_ff1), "l0_w_ff2": f(l0_w_ff2),
        "l1_w_in": f(l1_w_in), "l1_w_out": f(l1_w_out), "l1_w_ff1": f(l1_w_ff1), "l1_w_ff2": f(l1_w_ff2),
        "gate_gb": rep(l0_gate_ln_g, l0_gate_ln_b),
        "wsT": np.ascontiguousarray(np.transpose(f(l0_w_spatial), (2, 0, 1)).reshape(128, 512)),
        "bsT": np.ascontiguousarray(f(l0_b_spatial).T),
        "tt": _natten_tt(f(l0_na_rpb)),
        "l0_ln1": rep(l0_ln1_g, l0_ln1_b), "l0_ln2": rep(l0_ln2_g, l0_ln2_b),
        "l1_ln1": rep(l1_ln1_g, l1_ln1_b), "l1_ln2": rep(l1_ln2_g, l1_ln2_b),
        "lamv": rep(l1_lambda_q1, l1_lambda_k1, l1_lambda_q2, l1_lambda_k2),
        "subg": rep(l1_subln_g),
    }
    tabs = {typ: _attn_tables(typ) for typ in ("P", "S")}
    in_maps = []
    for c in range(8):
        typ = "P" if c < 4 else "S"
        xc = x_prompt[c] if c < 4 else x_sample[2 * (c - 4):2 * (c - 4) + 2].reshape(T, D)
        m = dict(shared)
        m["x"] = np.ascontiguousarray(xc)
        m["vb"] = vbt[typ]
        m["ab"], m["dg"], m["vb2"] = tabs[typ]
        in_maps.append(m)
    res = run_bass_kernel_spmd(nc, in_maps, core_ids=list(range(8)))
    if DEBUG:
        _CACHE["dbg"] = res.results
    ys = [np.asarray(r["y"], dtype=np.float32) for r in res.results]
    y_prompt = np.stack(ys[0:4], axis=0)
    y_sample = np.stack(ys[4:8], axis=0).reshape(8, 2048, D)
    return (y_prompt, y_sample)
```
